# Optimizing a Trainium2 kernel written in Bass

```python
import jax, jax.numpy as jnp
from jax import lax
import numpy as np

D_MODEL = 1024
BATCH = 2
SEQ = 8192
DEPTH = 1

MLA_HEADS = 8
MLA_Q_RANK = 256
MLA_KV_RANK = 128
MLA_NOPE_DIM = 64
MLA_ROPE_DIM = 32
MLA_V_DIM = 64
MLA_QK_DIM = MLA_NOPE_DIM + MLA_ROPE_DIM
Q_BLOCK = 128
RET_HEADS = 8
RET_QK_DIM = D_MODEL // (2 * RET_HEADS)
RET_V_DIM = 2 * RET_QK_DIM
RET_CHUNK = 128
FFN_HIDDEN = -(-8 * D_MODEL // (3 * 256)) * 256
ROPE_THETA = 10000.0
EPS = 1e-6

IN_SPLITS = [
    MLA_Q_RANK,
    MLA_KV_RANK,
    MLA_ROPE_DIM,
    RET_HEADS * RET_QK_DIM,
    RET_HEADS * RET_QK_DIM,
    RET_HEADS * RET_V_DIM,
    RET_HEADS * RET_V_DIM,
    2 * D_MODEL,
]
IN_WIDTH = sum(IN_SPLITS)

kernel_name = "hybrid_mla_retention_gated_block"


def _rms(xf):
    return xf * lax.rsqrt(jnp.mean(xf * xf, axis=-1, keepdims=True) + EPS)


def rms_norm(x, g):
    y = _rms(x.astype(jnp.float32)) * g.astype(jnp.float32)
    return y.astype(x.dtype)


def rope(x, positions):
    half = x.shape[-1] // 2
    inv = ROPE_THETA ** (-jnp.arange(half, dtype=jnp.float32) / half)
    ang = positions.astype(jnp.float32)[..., None] * inv
    cos = jnp.cos(ang)[:, :, None, :]
    sin = jnp.sin(ang)[:, :, None, :]
    xf = x.astype(jnp.float32)
    x1, x2 = xf[..., :half], xf[..., half:]
    out = jnp.concatenate([x1 * cos - x2 * sin, x2 * cos + x1 * sin], axis=-1)
    return out.astype(x.dtype)


def mla_attention(c_q, c_kv, k_rope, positions, g_q_a, w_q_b, g_kv_a, w_kv_b, g_qn, g_kn):
    B, S, _ = c_q.shape
    H = MLA_HEADS
    q = (rms_norm(c_q, g_q_a) @ w_q_b).reshape(B, S, H, MLA_QK_DIM)
    kv = (rms_norm(c_kv, g_kv_a) @ w_kv_b).reshape(B, S, H, MLA_NOPE_DIM + MLA_V_DIM)
    k_nope, v = kv[..., :MLA_NOPE_DIM], kv[..., MLA_NOPE_DIM:]
    k_r = jnp.broadcast_to(k_rope[:, :, None, :], (B, S, H, MLA_ROPE_DIM))
    k = jnp.concatenate([k_nope, k_r], axis=-1)
    q = rms_norm(q, g_qn)
    k = rms_norm(k, g_kn)
    q = jnp.concatenate([q[..., :MLA_NOPE_DIM], rope(q[..., MLA_NOPE_DIM:], positions)], axis=-1)
    k = jnp.concatenate([k[..., :MLA_NOPE_DIM], rope(k[..., MLA_NOPE_DIM:], positions)], axis=-1)
    q = q.astype(jnp.float32).transpose(0, 2, 1, 3)
    k = k.astype(jnp.float32).transpose(0, 2, 1, 3)
    v = v.astype(jnp.float32).transpose(0, 2, 1, 3)
    scale = MLA_QK_DIM ** -0.5
    nb = S // Q_BLOCK
    qb = q.reshape(B, H, nb, Q_BLOCK, MLA_QK_DIM).transpose(2, 0, 1, 3, 4)

    def attend(q_blk):
        s = jnp.einsum('bhqd,bhkd->bhqk', q_blk, k) * scale
        p = jax.nn.softmax(s, axis=-1)
        return jnp.einsum('bhqk,bhkv->bhqv', p, v)

    o = lax.map(attend, qb)
    o = o.transpose(1, 0, 3, 2, 4).reshape(B, S, H * MLA_V_DIM)
    return o


def retention_dir(q, k, v, log_gamma, strict):
    B, H, S, dk = q.shape
    dv = v.shape[-1]
    C = RET_CHUNK
    n = S // C
    idx = jnp.arange(C, dtype=jnp.float32)
    diff = idx[:, None] - idx[None, :]
    mask = diff > 0 if strict else diff >= 0
    decay_in = jnp.where(mask, jnp.exp(log_gamma[:, None, None] * jnp.maximum(diff, 0.0)), 0.0)
    q_decay = jnp.exp(log_gamma[:, None] * (idx + 1.0))[..., None]
    k_decay = jnp.exp(log_gamma[:, None] * (C - 1.0 - idx))[..., None]
    chunk_decay = jnp.exp(log_gamma * C)[:, None, None]

    def to_chunks(a):
        return a.reshape(B, H, n, C, a.shape[-1]).transpose(2, 0, 1, 3, 4)

    def step(state, inp):
        qi, ki, vi = inp
        inner = jnp.einsum('bhcd,bhed->bhce', qi, ki) * decay_in
        inner = jnp.einsum('bhce,bhev->bhcv', inner, vi)
        cross = jnp.einsum('bhcd,bhdv->bhcv', qi * q_decay, state)
        new_state = state * chunk_decay + jnp.einsum('bhcd,bhcv->bhdv', ki * k_decay, vi)
        return new_state, inner + cross

    state0 = jnp.zeros((B, H, dk, dv), jnp.float32)
    _, out = lax.scan(step, state0, (to_chunks(q), to_chunks(k), to_chunks(v)))
    return out.transpose(1, 2, 0, 3, 4).reshape(B, H, S, dv)


def bidirectional_retention(q, k, v, decay_fwd, decay_bwd):
    lg_f = -jnp.exp(decay_fwd.astype(jnp.float32))
    lg_b = -jnp.exp(decay_bwd.astype(jnp.float32))
    fwd = retention_dir(q, k, v, lg_f, False)
    flip = lambda a: jnp.flip(a, axis=2)
    bwd = flip(retention_dir(flip(q), flip(k), flip(v), lg_b, True))
    return fwd + bwd


def setup_inputs(seed: int = 0) -> dict:
    key = jax.random.key(seed)
    ks = jax.random.split(key, 20)
    f32 = jnp.float32

    def w(k, fan_in, fan_out):
        return jax.random.normal(k, (fan_in, fan_out), f32) * fan_in ** -0.5

    def gain(k, n):
        return 1.0 + 0.02 * jax.random.normal(k, (n,), f32)

    gamma0 = 1.0 - 2.0 ** (-5.0 - jnp.arange(RET_HEADS, dtype=f32))
    decay_base = jnp.log(-jnp.log(gamma0))
    x = jax.random.normal(ks[0], (BATCH, SEQ, D_MODEL), f32)
    positions = (jnp.arange(SEQ, dtype=jnp.int32)[None, :]
                 + jax.random.randint(ks[1], (BATCH, 1), 0, SEQ, dtype=jnp.int32))
    return {
        "x": x,
        "positions": positions,
        "g_mix": gain(ks[2], D_MODEL),
        "w_in": w(ks[3], D_MODEL, IN_WIDTH),
        "g_q_a": gain(ks[4], MLA_Q_RANK),
        "w_q_b": w(ks[5], MLA_Q_RANK, MLA_HEADS * MLA_QK_DIM),
        "g_kv_a": gain(ks[6], MLA_KV_RANK),
        "w_kv_b": w(ks[7], MLA_KV_RANK, MLA_HEADS * (MLA_NOPE_DIM + MLA_V_DIM)),
        "g_qn": gain(ks[8], MLA_QK_DIM),
        "g_kn": gain(ks[9], MLA_QK_DIM),
        "w_mla_out": w(ks[10], MLA_HEADS * MLA_V_DIM, D_MODEL),
        "ret_decay_fwd": decay_base + 0.05 * jax.random.normal(ks[11], (RET_HEADS,), f32),
        "ret_decay_bwd": decay_base + 0.05 * jax.random.normal(ks[12], (RET_HEADS,), f32),
        "w_ret_out": w(ks[13], RET_HEADS * RET_V_DIM, D_MODEL),
        "w_out": w(ks[14], D_MODEL, D_MODEL),
        "g_ffn": gain(ks[15], D_MODEL),
        "w_gate_up": w(ks[16], D_MODEL, 2 * FFN_HIDDEN),
        "w_down": w(ks[17], FFN_HIDDEN, D_MODEL),
    }


def reference(x, positions, g_mix, w_in, g_q_a, w_q_b, g_kv_a, w_kv_b, g_qn, g_kn,
              w_mla_out, ret_decay_fwd, ret_decay_bwd, w_ret_out, w_out,
              g_ffn, w_gate_up, w_down):
    B, S, D = x.shape
    split_idx = np.cumsum(IN_SPLITS)[:-1].tolist()
    for _ in range(DEPTH):
        h = rms_norm(x, g_mix)
        proj = h @ w_in
        c_q, c_kv, k_rope, q_r, k_r, v_r, g_r, gate_logits = jnp.split(proj, split_idx, axis=-1)

        o_a = mla_attention(c_q, c_kv, k_rope, positions, g_q_a, w_q_b, g_kv_a, w_kv_b, g_qn, g_kn)
        y_a = o_a.astype(x.dtype) @ w_mla_out

        q_r = rope(q_r.reshape(B, S, RET_HEADS, RET_QK_DIM), positions)
        k_r = rope(k_r.reshape(B, S, RET_HEADS, RET_QK_DIM), positions)
        q_r = q_r.astype(jnp.float32).transpose(0, 2, 1, 3)
        k_r = k_r.astype(jnp.float32).transpose(0, 2, 1, 3) * (RET_QK_DIM ** -0.5)
        v_r = v_r.reshape(B, S, RET_HEADS, RET_V_DIM).astype(jnp.float32).transpose(0, 2, 1, 3)
        ret = bidirectional_retention(q_r, k_r, v_r, ret_decay_fwd, ret_decay_bwd)
        ret = _rms(ret).transpose(0, 2, 1, 3).reshape(B, S, RET_HEADS * RET_V_DIM)
        o_b = (jax.nn.silu(g_r.astype(jnp.float32)) * ret).astype(x.dtype)
        y_b = o_b @ w_ret_out

        gates = jax.nn.sigmoid(gate_logits.astype(jnp.float32))
        merged = gates[..., :D] * y_a.astype(jnp.float32) + gates[..., D:] * y_b.astype(jnp.float32)
        x = x + merged.astype(x.dtype) @ w_out

        h2 = rms_norm(x, g_ffn)
        gu = h2 @ w_gate_up
        gate, up = gu[..., :FFN_HIDDEN], gu[..., FFN_HIDDEN:]
        x = x + (jax.nn.silu(gate) * up) @ w_down
    return x
```

```python
import numpy as np
import concourse.bass as bass
import concourse.mybir as mybir
from concourse.bass_utils import run_bass_kernel_spmd
from contextlib import ExitStack

F32 = mybir.dt.float32
BF16 = mybir.dt.bfloat16
I32 = mybir.dt.int32
AF = mybir.ActivationFunctionType
ALU = mybir.AluOpType

ENGS = ("pe", "act", "dve", "pool", "sp")
NDMASEM = 24
PI = float(np.pi)
EPS = 1e-6
THETA = 10000.0

D = 1024
NT = 8192
OWN = 2048
TB = 512
NB = NT // TB
NOB = OWN // TB
FH = 2816
NFC = FH // 128


class Buf:
    __slots__ = ("name", "w", "r", "excl")

    def __init__(self, name="", excl=False):
        self.name = name
        self.w = None
        self.r = []
        self.excl = excl


class Op:
    __slots__ = ("eng", "fn", "deps", "signal", "sem", "val", "is_dma")

    def __init__(self, eng, fn, is_dma=False):
        self.eng = eng
        self.fn = fn
        self.deps = []
        self.signal = False
        self.sem = None
        self.val = 0
        self.is_dma = is_dma


class Prog:
    def __init__(self, nc):
        self.nc = nc
        self.ops = {e: [] for e in ENGS}
        self.stack = ExitStack()
        self.dma_pending = []
        self.top = 16640
        self.limit = 229376
        self.nalloc = 0

    def sb(self, name, shape, dt):
        n = 1
        for s in shape[1:]:
            n *= s
        nbytes = n * (4 if dt in (F32, I32) else 2)
        nbytes = (nbytes + 63) // 64 * 64
        off = self.top
        self.top += nbytes
        assert self.top <= self.limit, "SBUF overflow at %s: %d" % (name, self.top)
        self.nalloc += 1
        return self.nc.alloc_sbuf_tensor_at("%s_%d" % (name, self.nalloc), list(shape), dt, offset=off)

    def at_kb(self, kb):
        self.barrier()
        self.top = 16640 + int(kb * 1024)

    def ps(self, name, shape, dt=F32):
        return self.stack.enter_context(self.nc.psum_tensor(name, list(shape), dt))

    def op(self, eng, fn, reads=(), writes=(), dma=False):
        o = Op(eng, fn, dma)
        deps = []
        xr = [b for b in reads if b.excl]
        if xr:
            reads = [b for b in reads if not b.excl]
            writes = list(writes) + [b for b in xr if b not in writes]
        for b in reads:
            if b.w is not None:
                deps.append(b.w)
        for b in writes:
            if b.w is not None:
                deps.append(b.w)
            deps.extend(b.r)
        for b in reads:
            b.r.append(o)
        for b in writes:
            b.w = o
            b.r = []
        seen = set()
        for d in deps:
            if d is o or id(d) in seen:
                continue
            seen.add(id(d))
            if d.eng == "pe" and eng == "pe" and not d.is_dma and not dma:
                continue
            o.deps.append(d)
            d.signal = True
        self.ops[eng].append(o)
        return o

    def dma(self, eng, out, in_, reads=(), writes=()):
        o = self.op(eng, lambda e: e.dma_start(out=out, in_=in_), reads, writes, dma=True)
        self.dma_pending.append(o)
        return o

    def barrier(self):
        last = [self.ops[e][-1] for e in ENGS if self.ops[e] and not self.ops[e][-1].is_dma
                and self.ops[e][-1].fn is not None]
        dmas = self.dma_pending
        self.dma_pending = []
        for e in ENGS:
            o = Op(e, None)
            for d in last + dmas:
                if d.eng == e and not d.is_dma:
                    continue
                o.deps.append(d)
                d.signal = True
            self.ops[e].append(o)

    def wait_ops(self, eng, ops):
        o = Op(eng, None)
        for d in ops:
            o.deps.append(d)
            d.signal = True
        self.ops[eng].append(o)
        return o

    def emit(self):
        nc = self.nc
        st = self.stack
        sems = {e: st.enter_context(nc.semaphore("s_" + e)) for e in ENGS}
        dsems = {e: [st.enter_context(nc.semaphore("d_%s_%d" % (e, i))) for i in range(NDMASEM)]
                 for e in ("sp", "pool", "act")}
        for e in ENGS:
            cnt = 0
            nd = 0
            for o in self.ops[e]:
                if o.is_dma:
                    o.sem = dsems[e][nd % NDMASEM]
                    o.val = 16 * (nd // NDMASEM + 1)
                    nd += 1
                elif o.signal:
                    cnt += 1
                    o.sem = sems[e]
                    o.val = cnt
        prog = self

        def run_engine(ename, eng):
            known = {}
            for o in prog.ops[ename]:
                waits = [(d.sem, d.val) for d in o.deps]
                if o.is_dma and o.val > 16:
                    waits.append((o.sem, o.val - 16))
                for (s, v) in waits:
                    key = id(s)
                    if known.get(key, 0) >= v:
                        continue
                    eng.wait_ge(s, v)
                    known[key] = v
                if o.fn is None:
                    continue
                ins = o.fn(eng)
                if o.is_dma:
                    ins.then_inc(o.sem, 16)
                elif o.signal:
                    ins.then_inc(o.sem, 1)

        with nc.Block() as block:
            @block.tensor
            def _(eng):
                run_engine("pe", eng)

            @block.scalar
            def _(eng):
                run_engine("act", eng)

            @block.vector
            def _(eng):
                run_engine("dve", eng)

            @block.gpsimd
            def _(eng):
                run_engine("pool", eng)

            @block.sync
            def _(eng):
                run_engine("sp", eng)


class RPool:
    def __init__(self, P, name, shape, dt, n):
        self.items = [(P.sb("%s%d" % (name, i), shape, dt), Buf("%s%d" % (name, i))) for i in range(n)]
        self.i = 0

    def get(self):
        it = self.items[self.i % len(self.items)]
        self.i += 1
        return it


class PsPool:
    def __init__(self, pairs):
        self.pairs = pairs
        self.i1 = 0
        self.i2 = 0

    def get1(self):
        n = len(self.pairs) * 2
        k = self.i1 % n
        self.i1 += 1
        t, b0, b1 = self.pairs[k // 2]
        return (t[:, 0:512], b0) if k % 2 == 0 else (t[:, 512:1024], b1)

    def get2(self):
        t, b0, b1 = self.pairs[self.i2 % len(self.pairs)]
        self.i2 += 1
        return t, b0, b1


def build_nc(debug=False, stop=None):
    nc = bass.Bass("TRN2", target_bir_lowering=False)

    def din(name, shape, dt=F32):
        return nc.dram_tensor(name, list(shape), dt, kind="ExternalInput").ap()

    xT = din("xT", [D, NT])
    posd = din("pos", [1, NT], I32)
    reld = din("rel", [128, 64])
    gmixd = din("gmix", [128, 8])
    gffnd = din("gffn", [128, 8])
    gqad = din("gqa", [128, 2])
    gkvad = din("gkva", [128, 1])
    gqaugd = din("gqaug", [128, 1])
    gkaugd = din("gkaug", [128, 1])
    decfd = din("decf", [8])
    decbd = din("decb", [8])
    wcqd = din("wcq", [128, 8 * 256])
    wckvd = din("wckv", [128, 8 * 128])
    wkrd = din("wkr", [128, 8 * 64])
    wqrd = din("wqr", [128, 8 * 1024])
    wkrrd = din("wkrr", [128, 8 * 1024])
    wvrd = din("wvr", [128, 8 * 1024])
    wgrd = din("wgr", [128, 8 * 1024])
    wkalld = din("wkall", [128, 8 * 1024])
    wvalld = din("wvall", [128, 8 * 1024])
    wgated = din("wgate", [128, 8 * 2048])
    wqbd = din("wqb", [128, 2 * 1024])
    wkvbkd = din("wkvbk", [128, 8 * 64])
    wkvbvd = din("wkvbv", [128, 8 * 64])
    wmlad = din("wmla", [128, 4 * 1024])
    wretd = din("wret", [128, 8 * 1024])
    woutd = din("wout", [128, 8 * 1024])
    wgud = din("wgu", [128, 8 * 2 * FH])
    wdownd = din("wdown", [128, NFC * 1024])
    yT = nc.dram_tensor("yT", [D, OWN], F32, kind="ExternalOutput").ap()
    x1s = nc.dram_tensor("x1s", [D, OWN], F32, kind="Internal").ap()

    P = Prog(nc)
    with P.stack:
        pairs = []
        for i in range(4):
            t = P.ps("psum%d" % i, [128, 1024], F32)
            pairs.append((t, Buf("ps%da" % i, True), Buf("ps%db" % i, True)))
        ps_all = PsPool(pairs)
        ps_lo = PsPool(pairs[0:2])

        def act(fn, reads, writes):
            return P.op("act", fn, reads, writes)

        def dve(fn, reads, writes):
            return P.op("dve", fn, reads, writes)

        def pool(fn, reads, writes):
            return P.op("pool", fn, reads, writes)

        def mm(out, pairs_, reads, wbuf):
            n = len(pairs_)
            for i, (l, r) in enumerate(pairs_):
                P.op("pe", lambda e, l=l, r=r, i=i: e.matmul(out, lhsT=l, rhs=r, start=(i == 0), stop=(i == n - 1)),
                     reads, [wbuf])

        def rstd(out, ps_in, n, eps, reads, wbuf):
            act(lambda e: e.activation(out=out, in_=ps_in, func=AF.Ln, scale=1.0 / n, bias=eps_t[0:ps_in.shape[0], eps:eps + 1]),
                list(reads) + [b_eps], [wbuf])
            act(lambda e: e.activation(out=out, in_=out, func=AF.Exp, scale=-0.5), [wbuf], [wbuf])

        b_const = Buf("const")
        b_eps = Buf("eps")
        eps_t = P.sb("eps_t", [128, 2], F32)
        dve(lambda e: e.memset(eps_t[:, 0:1], EPS), [], [b_eps])
        dve(lambda e: e.memset(eps_t[:, 1:2], 0.0), [], [b_eps])
        ones_bf = P.sb("ones_bf", [128, 128], BF16)
        maskq = P.sb("maskq", [128, 128], BF16)
        Jt = P.sb("Jt", [128, 128], BF16)
        JMt = P.sb("JMt", [128, 128], BF16)
        cf = P.sb("cf", [128, 128], F32)
        c1 = P.sb("c1", [128, 384], F32)
        dve(lambda e: e.memset(ones_bf[:], 1.0), [], [b_const])
        dve(lambda e: e.memset(maskq[:], 1.0), [], [b_const])
        dve(lambda e: e.memset(maskq[32:64, :], 0.0), [], [b_const])
        b_cf = Buf("cf")
        pool(lambda e: e.iota(cf[:], [[1, 128]], base=64, channel_multiplier=-1, allow_small_or_imprecise_dtypes=True),
             [], [b_cf])
        dve(lambda e: e.tensor_scalar(out=c1[:, 0:128], in0=cf[:], scalar1=64.0, scalar2=None, op0=ALU.is_equal), [b_cf], [b_cf])
        dve(lambda e: e.tensor_scalar(out=c1[:, 128:256], in0=cf[:], scalar1=0.0, scalar2=None, op0=ALU.is_equal), [b_cf], [b_cf])
        dve(lambda e: e.tensor_scalar(out=c1[:, 256:384], in0=cf[:], scalar1=128.0, scalar2=None, op0=ALU.is_equal), [b_cf], [b_cf])
        dve(lambda e: e.tensor_tensor(out=cf[:], in0=c1[:, 0:128], in1=c1[:, 128:256], op=ALU.add), [b_cf], [b_cf])
        dve(lambda e: e.tensor_tensor(out=cf[:], in0=cf[:], in1=c1[:, 256:384], op=ALU.add), [b_cf], [b_cf])
        dve(lambda e: e.tensor_copy(out=Jt[:], in_=cf[:]), [b_cf], [b_const])
        dve(lambda e: e.tensor_scalar(out=c1[:, 128:256], in0=c1[:, 128:256], scalar1=0.0, scalar2=None, op0=ALU.mult), [b_cf], [b_cf])
        pool(lambda e: e.iota(cf[:], [[1, 128]], base=64, channel_multiplier=-1, allow_small_or_imprecise_dtypes=True),
             [b_cf], [b_cf])
        dve(lambda e: e.tensor_scalar(out=c1[:, 256:288], in0=cf[:, 0:32], scalar1=32.0, scalar2=None, op0=ALU.is_equal), [b_cf], [b_cf])
        dve(lambda e: e.tensor_tensor(out=c1[:, 128:160], in0=c1[:, 0:32], in1=c1[:, 256:288], op=ALU.add), [b_cf], [b_cf])
        dve(lambda e: e.tensor_copy(out=c1[:, 192:256], in_=c1[:, 64:128]), [b_cf], [b_cf])
        dve(lambda e: e.tensor_copy(out=JMt[:], in_=c1[:, 128:256]), [b_cf], [b_const])

        gmix = P.sb("gmix", [128, 8], F32)
        gffn = P.sb("gffn", [128, 8], F32)
        gqa = P.sb("gqa", [128, 2], F32)
        gkva = P.sb("gkva", [128, 1], F32)
        gqaug = P.sb("gqaug", [128, 1], F32)
        gkaug = P.sb("gkaug", [128, 1], F32)
        for (t, d_) in ((gmix, gmixd), (gffn, gffnd), (gqa, gqad), (gkva, gkvad), (gqaug, gqaugd), (gkaug, gkaugd)):
            P.dma("sp", t[:], d_, [], [b_const])
        lgf = P.sb("lgf", [128, 8], F32)
        lgb = P.sb("lgb", [128, 8], F32)
        gCf = P.sb("gCf", [128, 8], F32)
        gCb = P.sb("gCb", [128, 8], F32)
        dkf = P.sb("dkf", [128, 8], F32)
        dkb = P.sb("dkb", [128, 8], F32)
        pidx = P.sb("pidx", [128, 2], F32)
        b_dec = Buf("dec")
        P.dma("sp", lgf[:], decfd.partition_broadcast(128), [], [b_dec])
        P.dma("sp", lgb[:], decbd.partition_broadcast(128), [], [b_dec])
        pool(lambda e: e.iota(pidx[:, 0:1], [[0, 1]], base=0, channel_multiplier=1, allow_small_or_imprecise_dtypes=True), [], [b_dec])
        pool(lambda e: e.iota(pidx[:, 1:2], [[0, 1]], base=127, channel_multiplier=-1, allow_small_or_imprecise_dtypes=True), [], [b_dec])
        for t in (lgf, lgb):
            act(lambda e, t=t: e.activation(out=t[:], in_=t[:], func=AF.Exp), [b_dec], [b_dec])
            dve(lambda e, t=t: e.tensor_scalar(out=t[:], in0=t[:], scalar1=-1.0, scalar2=None, op0=ALU.mult), [b_dec], [b_dec])
        act(lambda e: e.activation(out=gCf[:], in_=lgf[:], func=AF.Exp, scale=128.0), [b_dec], [b_dec])
        act(lambda e: e.activation(out=gCb[:], in_=lgb[:], func=AF.Exp, scale=128.0), [b_dec], [b_dec])
        act(lambda e: e.activation(out=dkf[:], in_=lgf[:], func=AF.Exp, scale=pidx[:, 1:2]), [b_dec], [b_dec])
        act(lambda e: e.activation(out=dkb[:], in_=lgb[:], func=AF.Exp, scale=pidx[:, 0:1]), [b_dec], [b_dec])

        diff = P.sb("diff", [128, 128], F32)
        dposm = P.sb("dposm", [128, 4, 128], F32)
        cp1 = P.sb("cp1", [128, 512], F32)
        cm = P.sb("cm", [128, 512], F32)
        pool(lambda e: e.iota(diff[:], [[1, 128]], base=0, channel_multiplier=-1, allow_small_or_imprecise_dtypes=True), [], [b_dec])
        pool(lambda e: e.iota(cp1[:].rearrange("p (a c) -> p a c", a=4), [[0, 4], [1, 128]], base=1, channel_multiplier=0,
                              allow_small_or_imprecise_dtypes=True), [], [b_dec])
        pool(lambda e: e.iota(cm[:].rearrange("p (a c) -> p a c", a=4), [[0, 4], [-1, 128]], base=128, channel_multiplier=0,
                              allow_small_or_imprecise_dtypes=True), [], [b_dec])
        dve(lambda e: e.tensor_scalar(out=dposm[:, 0, :], in0=diff[:], scalar1=0.0, scalar2=None, op0=ALU.max), [b_dec], [b_dec])
        dve(lambda e: e.tensor_scalar(out=dposm[:, 1, :], in0=diff[:], scalar1=-1.0, scalar2=0.0, op0=ALU.mult, op1=ALU.max), [b_dec], [b_dec])
        dve(lambda e: e.tensor_scalar(out=dposm[:, 2, :], in0=diff[:], scalar1=0.0, scalar2=None, op0=ALU.is_ge), [b_dec], [b_dec])
        dve(lambda e: e.tensor_scalar(out=dposm[:, 3, :], in0=diff[:], scalar1=0.0, scalar2=None, op0=ALU.is_lt), [b_dec], [b_dec])

        c2R = P.sb("c2R", [2, 128], F32)
        c2M = P.sb("c2M", [2, 128], F32)
        rw = P.sb("rw", [1, 512], F32)
        b_rw = Buf("rw")
        b_c2 = Buf("c2")
        pool(lambda e: e.iota(rw[:, 0:128].rearrange("p (a c) -> p a c", a=4), [[0, 4], [1, 32]], base=0, channel_multiplier=0,
                              allow_small_or_imprecise_dtypes=True), [], [b_rw])
        act(lambda e: e.activation(out=rw[:, 0:128], in_=rw[:, 0:128], func=AF.Exp, scale=-float(np.log(THETA)) / 32.0), [b_rw], [b_rw])
        dve(lambda e: e.memset(rw[:, 128:192], PI / 2), [], [b_rw])
        dve(lambda e: e.memset(rw[:, 192:224], PI), [], [b_rw])
        dve(lambda e: e.memset(rw[:, 224:256], 0.0), [], [b_rw])
        pool(lambda e: e.iota(rw[:, 256:320].rearrange("p (a c) -> p a c", a=4), [[0, 4], [1, 16]], base=0, channel_multiplier=0,
                              allow_small_or_imprecise_dtypes=True), [], [b_rw])
        act(lambda e: e.activation(out=rw[:, 256:320], in_=rw[:, 256:320], func=AF.Exp, scale=-float(np.log(THETA)) / 16.0), [b_rw], [b_rw])
        dve(lambda e: e.memset(rw[:, 320:384], 0.0), [], [b_rw])
        dve(lambda e: e.memset(rw[:, 384:416], PI / 2), [], [b_rw])
        dve(lambda e: e.memset(rw[:, 416:432], PI), [], [b_rw])
        dve(lambda e: e.memset(rw[:, 432:448], 0.0), [], [b_rw])
        dve(lambda e: e.memset(rw[:, 448:512], PI / 2), [], [b_rw])
        P.dma("sp", c2R[0:1, :], rw[:, 0:128], [b_rw], [b_c2])
        P.dma("sp", c2R[1:2, :], rw[:, 128:256], [b_rw], [b_c2])
        P.dma("sp", c2M[0:1, :], rw[:, 256:384], [b_rw], [b_c2])
        P.dma("sp", c2M[1:2, :], rw[:, 384:512], [b_rw], [b_c2])

        relt = P.sb("relt", [128, 64], F32)
        wfall = P.sb("wfall", [128, 8, 64], F32)
        wball = P.sb("wball", [128, 8, 64], F32)
        mfb = P.sb("mfb", [128, 2, 64], F32)
        etmp = P.sb("etmp", [128, 2, 64], F32)
        b_rel = Buf("rel")
        P.dma("sp", relt[:], reld, [], [b_rel])
        dve(lambda e: e.tensor_scalar(out=etmp[:, 0, :], in0=relt[:], scalar1=-1.0, scalar2=-1.0, op0=ALU.mult, op1=ALU.add), [b_rel], [b_rel])
        dve(lambda e: e.tensor_scalar(out=etmp[:, 0, :], in0=etmp[:, 0, :], scalar1=0.0, scalar2=None, op0=ALU.max), [b_rel], [b_rel])
        dve(lambda e: e.tensor_scalar(out=etmp[:, 1, :], in0=relt[:], scalar1=-float(OWN), scalar2=0.0, op0=ALU.add, op1=ALU.max), [b_rel], [b_rel])
        dve(lambda e: e.tensor_scalar(out=mfb[:, 0, :], in0=relt[:], scalar1=0.0, scalar2=0.125, op0=ALU.is_lt, op1=ALU.mult), [b_rel], [b_rel])
        dve(lambda e: e.tensor_scalar(out=mfb[:, 1, :], in0=relt[:], scalar1=float(OWN), scalar2=0.125, op0=ALU.is_ge, op1=ALU.mult), [b_rel], [b_rel])
        for h in range(8):
            act(lambda e, h=h: e.activation(out=wfall[:, h, :], in_=etmp[:, 0, :], func=AF.Exp, scale=lgf[:, h:h + 1]), [b_rel, b_dec], [b_rel])
            act(lambda e, h=h: e.activation(out=wball[:, h, :], in_=etmp[:, 1, :], func=AF.Exp, scale=lgb[:, h:h + 1]), [b_rel, b_dec], [b_rel])

        rstd_tm = P.sb("rstd_tm", [128, 64], F32)
        b_rtm = [Buf("rtm%d" % i) for i in range(NB)]
        assert P.top <= 16640 + 20 * 1024, P.top
        P.top = 16640 + 20 * 1024
        xg_own = P.sb("xg_own", [128, 8, OWN], BF16)
        b_xgown = [Buf("xgown%d" % i) for i in range(NOB)]
        rstdx_own = P.sb("rstdx_own", [128, OWN], F32)
        b_rstdx_own = [Buf("rstdxown%d" % i) for i in range(NOB)]
        OA = P.sb("OA", [128, 4, OWN], BF16)
        b_OA = [Buf("OA%d" % i) for i in range(NOB)]
        TF0 = P.sb("TF0", [128, 8, 128], F32)
        TB0 = P.sb("TB0", [128, 8, 128], F32)
        b_T0 = Buf("T0")
        ckvn = P.sb("ckvn", [128, NT], BF16)
        KRS = P.sb("KRS", [128, NT], BF16)
        cqn = P.sb("cqn", [128, 2, OWN], BF16)
        trigM_own = P.sb("trigM_own", [128, OWN], BF16)
        b_ckvn = [Buf("ckvn%d" % i) for i in range(NB)]
        b_KRS = [Buf("KRS%d" % i) for i in range(NB)]
        b_cqn = [Buf("cqn%d" % i) for i in range(NOB)]
        b_trigMo = [Buf("trigMo%d" % i) for i in range(NOB)]
        assert P.top == 16640 + 128 * 1024, P.top

        if stop == 'c':
            P.barrier()
            P.emit()
            return nc
        P.top = 16640 + 60 * 1024
        wkall = P.sb("wkall", [128, 8, 1024], BF16)
        P.top = 16640 + 128 * 1024
        wcq = P.sb("wcq", [128, 8, 256], BF16)
        wckv = P.sb("wckv", [128, 8, 128], BF16)
        wkr = P.sb("wkr", [128, 8, 64], BF16)
        wvall = P.sb("wvall", [128, 8, 1024], BF16)
        b_wcq, b_wckv, b_wkr, b_wkall, b_wvall = Buf("wcq"), Buf("wckv"), Buf("wkr"), Buf("wkall"), Buf("wvall")
        for (t, d_, bb) in ((wckv, wckvd, b_wckv), (wkr, wkrd, b_wkr), (wcq, wcqd, b_wcq), (wkall, wkalld, b_wkall), (wvall, wvalld, b_wvall)):
            P.dma("pool", t[:].rearrange("p a b -> p (a b)"), d_, [], [bb])

        xs_pool = RPool(P, "xs", [128, TB], F32, 3)
        sq_pool = RPool(P, "sq", [128, 8, TB], BF16, 1)
        xgb_pool = RPool(P, "xgb", [128, 8, TB], BF16, 2)
        f_pool = RPool(P, "ftmp", [128, TB], F32, 3)
        h_pool = RPool(P, "htmp", [128, TB], BF16, 3)
        rx_pool = RPool(P, "rx", [128, TB], F32, 1)
        trg_pool = RPool(P, "trg", [128, TB], BF16, 1)
        pr_pool = RPool(P, "posr", [2, TB], F32, 1)
        pi_pool = RPool(P, "posi", [1, TB], I32, 1)
        it_pool = RPool(P, "itmp", [128, TB], I32, 1)
        tt_pool = RPool(P, "ttok", [128, 1024], BF16, 1)
        vf_pool = RPool(P, "vf", [128, 1024], BF16, 1)
        vb_pool = RPool(P, "vb", [128, 1024], BF16, 1)
        trtb_pool = RPool(P, "trtb", [128, 4, 128], BF16, 2)
        sfb_pool = RPool(P, "sfb", [128, 2, 8, 4], F32, 2)
        bsb_pool = RPool(P, "bsb", [128, 2, 4], F32, 2)
        TFp, bTFa, bTFb = pairs[2]
        TBp, bTBa, bTBb = pairs[3]

        def trig_from_angle(out_ap, ang_ps, npart, reads, wbuf, width):
            it, b_it = it_pool.get()
            ft, b_ft = f_pool.get()
            dve(lambda e: e.tensor_scalar(out=it[0:npart, 0:width], in0=ang_ps, scalar1=1.0 / (2 * PI), scalar2=None, op0=ALU.mult),
                reads, [b_it])
            dve(lambda e: e.tensor_copy(out=ft[0:npart, 0:width], in_=it[0:npart, 0:width]), [b_it], [b_ft])
            dve(lambda e: e.scalar_tensor_tensor(out=ft[0:npart, 0:width], in0=ft[0:npart, 0:width], scalar=-2 * PI, in1=ang_ps,
                                                 op0=ALU.mult, op1=ALU.add), list(reads) + [b_ft], [b_ft])
            dve(lambda e: e.tensor_scalar(out=ft[0:npart, 0:width], in0=ft[0:npart, 0:width], scalar1=-3.1415925, scalar2=3.1415925,
                                          op0=ALU.max, op1=ALU.min), [b_ft], [b_ft])
            act(lambda e: e.activation(out=out_ap, in_=ft[0:npart, 0:width], func=AF.Sin), [b_ft], [wbuf])

        n_nonown_tiles = (NB - NOB) * 4
        half_bufs = {}
        pending_T = []
        tile_ctr = [0]
        ctx = {}

        class _Banks:
            def __init__(self, items):
                self.items = items
                self.i = 0

            def get1(self):
                it = self.items[self.i % len(self.items)]
                self.i += 1
                return it
        ps_tile = _Banks([(pairs[0][0][:, 0:512], pairs[0][1]), (pairs[0][0][:, 512:1024], pairs[0][2]),
                          (pairs[1][0][:, 0:512], pairs[1][1])])
        ps_ch = _Banks([(pairs[1][0][:, 512:1024], pairs[1][2])])

        def chain(b):
            own = b < NOB
            c0 = b * TB
            sq, b_sq = sq_pool.get()
            if own:
                xg = xg_own[:, :, c0:c0 + TB]
                b_xg = b_xgown[b]
            else:
                xgt, b_xg = xgb_pool.get()
                xg = xgt[:, :, :]
            for k in range(8):
                xs, b_xs = xs_pool.get()
                P.dma("sp", xs[:], xT[k * 128:(k + 1) * 128, c0:c0 + TB], [], [b_xs])
                act(lambda e, xs=xs, sq=sq, k=k: e.activation(out=sq[:, k, :], in_=xs[:], func=AF.Square), [b_xs], [b_sq])
                dve(lambda e, xs=xs, xg=xg, k=k: e.tensor_scalar(out=xg[:, k, :], in0=xs[:], scalar1=gmix[:, k:k + 1], scalar2=None,
                                                                op0=ALU.mult), [b_xs, b_const], [b_xg])
                if k % 2 == 1:
                    yield
            yield
            ps, b_ps = ps_ch.get1()
            mm(ps, [(ones_bf[:], sq[:, k, :]) for k in range(8)], [b_sq, b_const], b_ps)
            yield
            if own:
                rx = rstdx_own[:, c0:c0 + TB]
                b_rx = b_rstdx_own[b]
            else:
                rxt, b_rx = rx_pool.get()
                rx = rxt[:]
            rstd(rx, ps, float(D), 0, [b_ps], b_rx)
            yield
            ps, b_ps = ps_ch.get1()
            for j in range(4):
                mm(ps[:, j:j + 1], [(sq[:, k, j * 128:(j + 1) * 128], ones_bf[:, 0:1]) for k in range(8)], [b_sq, b_const], b_ps)
            yield
            rstd(rstd_tm[:, b * 4:b * 4 + 4], ps[:, 0:4], float(D), 0, [b_ps], b_rtm[b])
            pit, b_pi = pi_pool.get()
            prt, b_pr = pr_pool.get()
            P.dma("sp", pit[:], posd[0:1, c0:c0 + TB], [], [b_pi])
            dve(lambda e, prt=prt: e.memset(prt[:], 1.0), [], [b_pr])
            dve(lambda e, prt=prt, pit=pit: e.tensor_copy(out=prt[0:1, :], in_=pit[:]), [b_pi], [b_pr])
            yield
            ps, b_ps = ps_ch.get1()
            mm(ps, [(c2M[:], prt[:])], [b_c2, b_pr], b_ps)
            yield
            if own:
                trg = trigM_own[:, c0:c0 + TB]
                b_trg = b_trigMo[b]
            else:
                trgt, b_trg = trg_pool.get()
                trg = trgt[:]
            trig_from_angle(trg, ps, 128, [b_ps], b_trg, TB)
            yield
            ps, b_ps = ps_ch.get1()
            mm(ps, [(wckv[:, k, :], xg[:, k, :]) for k in range(8)], [b_wckv, b_xg], b_ps)
            yield
            ckv, b_ckv = f_pool.get()
            dve(lambda e, ckv=ckv, ps=ps, rx=rx: e.tensor_tensor(out=ckv[:], in0=ps, in1=rx, op=ALU.mult), [b_ps, b_rx], [b_ckv])
            sq2, b_sq2 = h_pool.get()
            act(lambda e, sq2=sq2, ckv=ckv: e.activation(out=sq2[:], in_=ckv[:], func=AF.Square), [b_ckv], [b_sq2])
            yield
            yield
            ps, b_ps = ps_ch.get1()
            mm(ps, [(ones_bf[:], sq2[:])], [b_sq2, b_const], b_ps)
            yield
            rkv, b_rkv = f_pool.get()
            rstd(rkv[:], ps, 128.0, 0, [b_ps], b_rkv)
            dve(lambda e, ckv=ckv, rkv=rkv, c0=c0: e.scalar_tensor_tensor(out=ckvn[:, c0:c0 + TB], in0=ckv[:], scalar=gkva[:, 0:1], in1=rkv[:],
                                                                          op0=ALU.mult, op1=ALU.mult), [b_ckv, b_rkv, b_const], [b_ckvn[b]])
            yield
            ps, b_ps = ps_ch.get1()
            mm(ps[0:64, :], [(wkr[:, k, :], xg[:, k, :]) for k in range(8)], [b_wkr, b_xg], b_ps)
            yield
            krw, b_krw = f_pool.get()
            dve(lambda e, krw=krw, ps=ps, rx=rx: e.tensor_tensor(out=krw[0:32, :], in0=ps[0:32, :], in1=rx[0:32, :], op=ALU.mult),
                [b_ps, b_rx], [b_krw])
            act(lambda e, krw=krw, c0=c0: e.activation(out=KRS[32:64, c0:c0 + TB], in_=krw[0:32, :], func=AF.Square), [b_krw], [b_KRS[b]])
            kt_, b_kt = h_pool.get()
            dve(lambda e, kt_=kt_, ps=ps, trg=trg: e.scalar_tensor_tensor(out=kt_[0:64, :], in0=ps[0:64, :], scalar=gkaug[0:64, 0:1],
                                                                           in1=trg[0:64, :], op0=ALU.mult, op1=ALU.mult),
                [b_ps, b_trg, b_const], [b_kt])
            yield
            yield
            ps2, b_ps2 = ps_ch.get1()
            mm(ps2[0:32, :], [(JMt[0:64, 0:32], kt_[0:64, :])], [b_kt, b_const], b_ps2)
            yield
            dve(lambda e, ps2=ps2, rx=rx, c0=c0: e.tensor_tensor(out=KRS[0:32, c0:c0 + TB], in0=ps2[0:32, :], in1=rx[0:32, :], op=ALU.mult),
                [b_ps2, b_rx], [b_KRS[b]])
            yield
            if own:
                cqf = []
                for c in range(2):
                    ps, b_ps = ps_ch.get1()
                    mm(ps, [(wcq[:, k, c * 128:(c + 1) * 128], xg[:, k, :]) for k in range(8)], [b_wcq, b_xg], b_ps)
                    yield
                    cq, b_cq = f_pool.get()
                    dve(lambda e, cq=cq, ps=ps, rx=rx: e.tensor_tensor(out=cq[:], in0=ps, in1=rx, op=ALU.mult), [b_ps, b_rx], [b_cq])
                    cqf.append((cq, b_cq))
                    yield
                sqs = []
                for c in range(2):
                    s_, b_s = h_pool.get()
                    act(lambda e, s_=s_, cq=cqf[c][0]: e.activation(out=s_[:], in_=cq[:], func=AF.Square), [cqf[c][1]], [b_s])
                    sqs.append((s_, b_s))
                yield
                yield
                ps, b_ps = ps_ch.get1()
                mm(ps, [(ones_bf[:], sqs[c][0][:]) for c in range(2)], [sqs[0][1], sqs[1][1], b_const], b_ps)
                yield
                rq, b_rq = f_pool.get()
                rstd(rq[:], ps, 256.0, 0, [b_ps], b_rq)
                for c in range(2):
                    dve(lambda e, c=c, cq=cqf[c][0], rq=rq, c0=c0: e.scalar_tensor_tensor(
                        out=cqn[:, c, c0:c0 + TB], in0=cq[:], scalar=gqa[:, c:c + 1], in1=rq[:], op0=ALU.mult, op1=ALU.mult),
                        [cqf[c][1], b_rq, b_const], [b_cqn[b]])
                yield
            else:
                bs, b_bs = bsb_pool.get()
                sfb, b_sfb = sfb_pool.get()
                dve(lambda e, bs=bs, b=b: e.tensor_tensor(out=bs[:, 0, :], in0=rstd_tm[:, b * 4:b * 4 + 4], in1=rstd_tm[:, b * 4:b * 4 + 4],
                                                          op=ALU.mult), [b_rtm[b]], [b_bs])
                dve(lambda e, bs=bs, b=b: e.tensor_tensor(out=bs[:, 1, :], in0=bs[:, 0, :], in1=mfb[:, 1, b * 4:b * 4 + 4], op=ALU.mult),
                    [b_bs, b_rel], [b_bs])
                dve(lambda e, bs=bs, b=b: e.tensor_tensor(out=bs[:, 0, :], in0=bs[:, 0, :], in1=mfb[:, 0, b * 4:b * 4 + 4], op=ALU.mult),
                    [b_bs, b_rel], [b_bs])
                dve(lambda e, bs=bs, sfb=sfb, b=b: e.tensor_tensor(out=sfb[:, 0, :, :], in0=wfall[:, :, b * 4:b * 4 + 4],
                                                                   in1=bs[:, 0, :].unsqueeze(1).broadcast_to([128, 8, 4]), op=ALU.mult),
                    [b_bs, b_rel], [b_sfb])
                dve(lambda e, bs=bs, sfb=sfb, b=b: e.tensor_tensor(out=sfb[:, 1, :, :], in0=wball[:, :, b * 4:b * 4 + 4],
                                                                   in1=bs[:, 1, :].unsqueeze(1).broadcast_to([128, 8, 4]), op=ALU.mult),
                    [b_bs, b_rel], [b_sfb])
                yield
                trtb, b_trtb = trtb_pool.get()
                for j in range(4):
                    psa, b_psa = ps_ch.get1()
                    mm(psa[:, 0:128], [(prt[:, j * 128:(j + 1) * 128], c2R[:])], [b_c2, b_pr], b_psa)
                    yield
                    trig_from_angle(trtb[:, j, :], psa[:, 0:128], 128, [b_psa], b_trtb, 128)
                    yield
                ctx[b] = (xg, b_xg, sfb, b_sfb, trtb, b_trtb)

        gens = []

        def adv(n=1):
            for _ in range(n):
                while gens:
                    try:
                        next(gens[0])
                        break
                    except StopIteration:
                        gens.pop(0)

        def tiles(b):
            xg, b_xg, sfb, b_sfb, trtb, b_trtb = ctx.pop(b)
            for j in range(4):
                tc_ = tile_ctr[0]
                tile_ctr[0] += 1
                cs = slice(j * 128, (j + 1) * 128)
                ttok, _ = tt_pool.get()
                vf, _ = vf_pool.get()
                vb, _ = vb_pool.get()
                hb = [half_bufs.setdefault((id(t_), hh), Buf("half")) for t_ in (ttok, vf, vb) for hh in range(2)]
                b_tth, b_vfh, b_vbh = hb[0:2], hb[2:4], hb[4:6]
                first = tc_ == 0
                last = tc_ == n_nonown_tiles - 1

                def emit_T(hrange, ttok=ttok, vf=vf, vb=vb, b_tth=b_tth, b_vfh=b_vfh, b_vbh=b_vbh, first=first, last=last):
                    for h in hrange:
                        hs = slice(h * 128, (h + 1) * 128)
                        st_ = first and (h % 4 == 0)
                        P.op("pe", lambda e, hs=hs, st_=st_: e.matmul(
                            TFp[:, hs], lhsT=ttok[:, hs], rhs=vf[:, hs], start=st_, stop=last, skip_group_check=True),
                            [b_tth[h // 4], b_vfh[h // 4]], [bTFa if h < 4 else bTFb])
                        P.op("pe", lambda e, hs=hs, st_=st_: e.matmul(
                            TBp[:, hs], lhsT=ttok[:, hs], rhs=vb[:, hs], start=st_, stop=last, skip_group_check=True),
                            [b_tth[h // 4], b_vbh[h // 4]], [bTBa if h < 4 else bTBb])

                for hh in range(2):
                    psk, b_psk = ps_tile.get1()
                    mm(psk, [(xg[:, k, cs], wkall[:, k, hh * 512:(hh + 1) * 512]) for k in range(8)], [b_wkall, b_xg], b_psk)
                    dve(lambda e, ttok=ttok, psk=psk, hh=hh, j=j: e.tensor_tensor(
                        out=ttok[:, hh * 512:(hh + 1) * 512].rearrange("p (h c) -> p h c", h=4),
                        in0=psk.rearrange("p (h c) -> p h c", h=4),
                        in1=trtb[:, j, :].unsqueeze(1).broadcast_to([128, 4, 128]), op=ALU.mult), [b_psk, b_trtb], [b_tth[hh]])
                    adv()
                    psv, b_psv = ps_tile.get1()
                    mm(psv, [(xg[:, k, cs], wvall[:, k, hh * 512:(hh + 1) * 512]) for k in range(8)], [b_wvall, b_xg], b_psv)
                    dve(lambda e, vf=vf, psv=psv, hh=hh, j=j: e.tensor_tensor(
                        out=vf[:, hh * 512:(hh + 1) * 512].rearrange("p (h c) -> p h c", h=4),
                        in0=psv.rearrange("p (h c) -> p h c", h=4),
                        in1=sfb[:, 0, hh * 4:(hh + 1) * 4, j:j + 1].broadcast_to([128, 4, 128]), op=ALU.mult), [b_psv, b_sfb], [b_vfh[hh]])
                    dve(lambda e, vb=vb, psv=psv, hh=hh, j=j: e.tensor_tensor(
                        out=vb[:, hh * 512:(hh + 1) * 512].rearrange("p (h c) -> p h c", h=4),
                        in0=psv.rearrange("p (h c) -> p h c", h=4),
                        in1=sfb[:, 1, hh * 4:(hh + 1) * 4, j:j + 1].broadcast_to([128, 4, 128]), op=ALU.mult), [b_psv, b_sfb], [b_vbh[hh]])
                    adv()
                    if hh == 0 and pending_T:
                        pending_T.pop()()
                        adv()
                emit_T(range(0, 4))
                adv()
                pending_T.append(lambda emit_T=emit_T: emit_T(range(4, 8)))

        hide = {4: [0, 5], 5: [1, 6], 6: [2, 7], 7: [3, 8]}
        for b_ in range(8, 15):
            hide[b_] = [b_ + 1]
        hide[15] = []
        for _ in chain(4):
            pass
        for b in range(NOB, NB):
            for x_ in hide[b]:
                gens.append(chain(x_))
            tiles(b)
            while gens:
                adv()
        while pending_T:
            pending_T.pop()()
        act(lambda e: e.copy(out=TF0[:].rearrange("p h c -> p (h c)"), in_=TFp[:]), [bTFa, bTFb], [b_T0])
        act(lambda e: e.copy(out=TB0[:].rearrange("p h c -> p (h c)"), in_=TBp[:]), [bTBa, bTBb], [b_T0])

        if stop == '1':
            P.barrier()
            P.emit()
            return nc
        P.at_kb(128)
        wqb = P.sb("wqb", [128, 2, 1024], BF16)
        wkvbk = P.sb("wkvbk", [128, 8, 128], BF16)
        wkvbv = P.sb("wkvbv", [128, 8, 64], BF16)
        b_w3 = Buf("w3")
        P.dma("pool", wqb[:].rearrange("p a b -> p (a b)"), wqbd, [], [b_w3])
        dve(lambda e: e.memset(wkvbk[:], 0.0), [], [b_w3])
        P.dma("pool", wkvbk[:, :, 64:128], wkvbkd.rearrange("p (h c) -> p h c", h=8), [], [b_w3])
        P.dma("pool", wkvbv[:].rearrange("p a b -> p (a b)"), wkvbvd, [], [b_w3])
        Kh2 = [P.sb("Kh%d" % i, [128, NT], BF16) for i in range(2)]
        Qh2 = [P.sb("Qh%d" % i, [128, OWN], BF16) for i in range(2)]
        Vh = P.sb("Vh", [128, 64, 128], BF16)
        b_Kh2 = [Buf("Kh0"), Buf("Kh1")]
        b_Qh2 = [Buf("Qh0"), Buf("Qh1")]
        b_Vh = Buf("Vh")
        for i in range(2):
            dve(lambda e, i=i: e.memset(Kh2[i][32:64, :], 0.0), [], [b_Kh2[i]])
        dve(lambda e: e.memset(Vh[:, :, 64:128], 1.0), [], [b_Vh])
        f_pool = RPool(P, "ftmp3", [128, TB], F32, 3)
        h_pool = RPool(P, "htmp3", [128, TB], BF16, 3)
        pt_pool = RPool(P, "pt", [128, 1024], BF16, 3)
        oh_pool = RPool(P, "oh", [64, TB], BF16, 1)
        SCALE = 96.0 ** -0.5
        ps_sc = PsPool(pairs[0:3])

        class _One:
            def get1(self):
                return pairs[3][0][:, 512:1024], pairs[3][2]
        ps_pr = _One()
        po_bank = (pairs[3][0][:, 0:512], pairs[3][1])

        def prep_head(h):
            Kh = Kh2[h % 2]
            b_Kh = b_Kh2[h % 2]
            Qh = Qh2[h % 2]
            b_Qh = b_Qh2[h % 2]
            for b in range(NB):
                c0 = b * TB
                ps, b_ps = ps_pr.get1()
                mm(ps, [(wkvbk[:, h, :], ckvn[:, c0:c0 + TB])], [b_w3, b_ckvn[b]], b_ps)
                kn, b_kn = f_pool.get()
                dve(lambda e, kn=kn, ps=ps: e.tensor_copy(out=kn[64:128, :], in_=ps[64:128, :]), [b_ps], [b_kn])
                sqk, b_sqk = h_pool.get()
                pool(lambda e, sqk=sqk, kn=kn: e.tensor_tensor(out=sqk[64:128, :], in0=kn[64:128, :], in1=kn[64:128, :], op=ALU.mult),
                     [b_kn], [b_sqk])
                yield
                yield
                yield
                ps2, b_ps2 = ps_pr.get1()
                mm(ps2, [(ones_bf[64:128, :], sqk[64:128, :]), (ones_bf[32:64, :], KRS[32:64, c0:c0 + TB])],
                   [b_sqk, b_KRS[b], b_const], b_ps2)
                yield
                yield
                rk, b_rk = f_pool.get()
                rstd(rk[:], ps2, 96.0, 0, [b_ps2], b_rk)
                dve(lambda e, kn=kn, rk=rk, c0=c0, Kh=Kh: e.scalar_tensor_tensor(out=Kh[64:128, c0:c0 + TB], in0=kn[64:128, :],
                                                                                 scalar=gkaug[64:128, 0:1], in1=rk[64:128, :],
                                                                                 op0=ALU.mult, op1=ALU.mult),
                    [b_kn, b_rk, b_const], [b_Kh])
                dve(lambda e, rk=rk, c0=c0, Kh=Kh: e.tensor_tensor(out=Kh[0:32, c0:c0 + TB], in0=KRS[0:32, c0:c0 + TB], in1=rk[0:32, :],
                                                                   op=ALU.mult), [b_KRS[b], b_rk], [b_Kh])
                yield
            for b in range(NOB):
                c0 = b * TB
                ps, b_ps = ps_pr.get1()
                mm(ps, [(wqb[:, k, h * 128:(h + 1) * 128], cqn[:, k, c0:c0 + TB]) for k in range(2)], [b_w3, b_cqn[b]], b_ps)
                qa, b_qa = f_pool.get()
                dve(lambda e, qa=qa, ps=ps: e.tensor_copy(out=qa[:], in_=ps), [b_ps], [b_qa])
                sqq, b_sqq = h_pool.get()
                pool(lambda e, sqq=sqq, qa=qa: e.tensor_tensor(out=sqq[:], in0=qa[:], in1=qa[:], op=ALU.mult), [b_qa], [b_sqq])
                tq, b_tq = h_pool.get()
                dve(lambda e, tq=tq, qa=qa, c0=c0: e.scalar_tensor_tensor(out=tq[:], in0=qa[:], scalar=gqaug[:, 0:1], in1=trigM_own[:, c0:c0 + TB],
                                                                          op0=ALU.mult, op1=ALU.mult), [b_qa, b_trigMo[b], b_const], [b_tq])
                yield
                yield
                ps2, b_ps2 = ps_pr.get1()
                mm(ps2, [(maskq[:], sqq[:])], [b_sqq, b_const], b_ps2)
                yield
                yield
                rq, b_rq = f_pool.get()
                rstd(rq[:], ps2, 96.0, 0, [b_ps2], b_rq)
                yield
                yield
                ps3, b_ps3 = ps_pr.get1()
                mm(ps3, [(JMt[:], tq[:])], [b_tq, b_const], b_ps3)
                dve(lambda e, ps3=ps3, rq=rq, c0=c0, Qh=Qh: e.scalar_tensor_tensor(out=Qh[:, c0:c0 + TB], in0=ps3, scalar=SCALE, in1=rq[:],
                                                                                   op0=ALU.mult, op1=ALU.mult), [b_ps3, b_rq], [b_Qh])
                yield

        def prep_v(h):
            for g in range(8):
                ps, b_ps = ps_pr.get1()
                for j in range(8):
                    kt = g * 8 + j
                    mm(ps[:, j * 64:(j + 1) * 64], [(ckvn[:, kt * 128:(kt + 1) * 128], wkvbv[:, h, :])], [b_w3, b_ckvn[kt // 4]], b_ps)
                dve(lambda e, ps=ps, g=g: e.tensor_copy(out=Vh[:, g * 8:(g + 1) * 8, 0:64], in_=ps.rearrange("p (j c) -> p j c", j=8)),
                    [b_ps], [b_Vh])

        for _ in prep_head(0):
            pass
        for h in range(8):
            prep_v(h)
            Kh = Kh2[h % 2]
            b_Kh = b_Kh2[h % 2]
            Qh = Qh2[h % 2]
            b_Qh = b_Qh2[h % 2]
            nxt = prep_head(h + 1) if h < 7 else None
            items = [(qb, kp) for qb in range(NOB) for kp in range(32)]
            scs = {}

            def issue_scores(idx):
                qb, kp = items[idx]
                sc, b_s0, b_s1 = ps_sc.get2()
                for i, bsx in enumerate((b_s0, b_s1)):
                    kt = kp * 2 + i
                    mm(sc[:, i * 512:(i + 1) * 512], [(Kh[:, kt * 128:(kt + 1) * 128], Qh[:, qb * TB:(qb + 1) * TB])], [b_Kh, b_Qh], bsx)
                scs[idx] = (sc, b_s0, b_s1)

            issue_scores(0)
            issue_scores(1)
            for idx in range(len(items)):
                if idx + 2 < len(items):
                    issue_scores(idx + 2)
                qb, kp = items[idx]
                q0 = qb * TB
                po, b_po = po_bank
                sc, b_s0, b_s1 = scs.pop(idx)
                pt, b_pt = pt_pool.get()
                act(lambda e, pt=pt, sc=sc: e.activation(out=pt[:], in_=sc[:], func=AF.Exp), [b_s0, b_s1], [b_pt])
                for i in range(2):
                    kt = kp * 2 + i
                    P.op("pe", lambda e, po=po, kt=kt, pt=pt, i=i: e.matmul(po, lhsT=Vh[:, kt, :], rhs=pt[:, i * 512:(i + 1) * 512],
                                                                             start=(kt == 0), stop=(kt == 63)), [b_Vh, b_pt], [b_po])
                if kp == 31:
                    rec, b_rec = f_pool.get()
                    dve(lambda e, rec=rec, po=po: e.reciprocal(out=rec[0:64, :], in_=po[64:128, :]), [b_po], [b_rec])
                    if h % 2 == 0:
                        dve(lambda e, rec=rec, po=po, h=h, q0=q0: e.tensor_tensor(out=OA[0:64, h // 2, q0:q0 + TB], in0=po[0:64, :],
                                                                                  in1=rec[0:64, :], op=ALU.mult), [b_po, b_rec], [b_OA[qb]])
                    else:
                        oh, b_oh = oh_pool.get()
                        dve(lambda e, rec=rec, po=po, oh=oh: e.tensor_tensor(out=oh[:], in0=po[0:64, :], in1=rec[0:64, :], op=ALU.mult),
                            [b_po, b_rec], [b_oh])
                        dve(lambda e, oh=oh, h=h, q0=q0: e.tensor_copy(out=OA[64:128, h // 2, q0:q0 + TB], in_=oh[:]), [b_oh], [b_OA[qb]])
                if nxt is not None and idx >= 2:
                    next(nxt, None)
            if nxt is not None:
                for _ in nxt:
                    pass
        if stop == '3':
            P.barrier()
            P.emit()
            return nc
        P.at_kb(84)
        OB = P.sb("OB", [128, 8, OWN], BF16)
        b_OB = [Buf("OB%d" % i) for i in range(NOB)]
        trigR = P.sb("trigR", [128, OWN], BF16)
        trigT = P.sb("trigT", [128, 16, 128], BF16)
        b_trigR = [Buf("trigR%d" % i) for i in range(NOB)]
        b_trigT = [Buf("trigT%d" % i) for i in range(NOB)]
        f_pool = RPool(P, "ftmp2", [128, TB], F32, 4)
        h_pool = RPool(P, "htmp2", [128, TB], BF16, 4)
        it_pool = RPool(P, "itmp2", [128, TB], I32, 1)
        pr_pool = RPool(P, "posr2", [2, TB], F32, 1)
        pi_pool = RPool(P, "posi2", [1, TB], I32, 1)
        for b in range(NOB):
            c0 = b * TB
            pit, b_pi = pi_pool.get()
            prt, b_pr = pr_pool.get()
            P.dma("sp", pit[:], posd[0:1, c0:c0 + TB], [], [b_pi])
            dve(lambda e, prt=prt: e.memset(prt[:], 1.0), [], [b_pr])
            dve(lambda e, prt=prt, pit=pit: e.tensor_copy(out=prt[0:1, :], in_=pit[:]), [b_pi], [b_pr])
            ps, b_ps = ps_all.get1()
            mm(ps, [(c2R[:], prt[:])], [b_c2, b_pr], b_ps)
            trf, b_trf = f_pool.get()
            trig_from_angle(trf[:], ps, 128, [b_ps], b_trf, TB)
            dve(lambda e, c0=c0, trf=trf: e.tensor_tensor(out=trigR[:, c0:c0 + TB], in0=trf[:], in1=rstdx_own[:, c0:c0 + TB], op=ALU.mult),
                [b_trf, b_rstdx_own[b]], [b_trigR[b]])
            for j in range(4):
                ps, b_ps = ps_all.get1()
                mm(ps[:, 0:128], [(prt[:, j * 128:(j + 1) * 128], c2R[:])], [b_c2, b_pr], b_ps)
                trig_from_angle(trigT[:, b * 4 + j, :], ps[:, 0:128], 128, [b_ps], b_trigT[b], 128)
        wq_p = RPool(P, "wq_h", [128, 8, 128], BF16, 2)
        wk_p = RPool(P, "wk_h", [128, 8, 128], BF16, 2)
        wv_p = RPool(P, "wv_h", [128, 8, 128], BF16, 1)
        wg_p = RPool(P, "wg_h", [128, 8, 128], BF16, 1)
        QQ = P.sb("QQ", [128, OWN], BF16)
        QQF = P.sb("QQF", [128, OWN], BF16)
        QQB = P.sb("QQB", [128, OWN], BF16)
        KK = P.sb("KK", [128, OWN], BF16)
        SG = P.sb("SG", [128, OWN], BF16)
        Vt = P.sb("Vt", [128, 16, 128], BF16)
        VFt = P.sb("VFt", [128, 16, 128], BF16)
        VBt = P.sb("VBt", [128, 16, 128], BF16)
        TTt = P.sb("TTt", [128, 16, 128], BF16)
        SFb = P.sb("SFb", [128, 16, 128], BF16)
        SBb = P.sb("SBb", [128, 16, 128], BF16)
        Dh = P.sb("Dh", [128, 128], F32)
        dtmp = P.sb("dtmp", [128, 2, 128], F32)
        qdF = P.sb("qdF", [128, TB], F32)
        qdB = P.sb("qdB", [128, TB], F32)
        scol = P.sb("scol", [128, 3, 16], F32)
        stF = [P.sb("stF%d" % i, [128, 128], F32) for i in range(2)]
        stB = [P.sb("stB%d" % i, [128, 128], F32) for i in range(2)]
        ad_pool = RPool(P, "ad", [128, 128], BF16, 4)
        b_QQ = [Buf("QQ%d" % i) for i in range(NOB)]
        b_KK = [Buf("KK%d" % i) for i in range(NOB)]
        b_SG = [Buf("SG%d" % i) for i in range(NOB)]
        b_Vt = [Buf("Vt%d" % i) for i in range(16)]
        b_SF = [Buf("SF%d" % i) for i in range(16)]
        b_SB = [Buf("SB%d" % i) for i in range(16)]
        b_Dh = Buf("Dh")
        b_qd = Buf("qd")
        b_scol = Buf("scol")
        b_stF = [Buf("stF0"), Buf("stF1")]
        b_stB = [Buf("stB0"), Buf("stB1")]
        ps_ret = PsPool(pairs[0:3])
        po_ret = [(pairs[3][0][:, 0:512], pairs[3][1]), (pairs[3][0][:, 512:1024], pairs[3][2])]
        for h in range(8):
            wq, b_wq = wq_p.get()
            wk, b_wk = wk_p.get()
            wv, b_wv = wv_p.get()
            wg, b_wg = wg_p.get()
            for (t, bb, d_) in ((wq, b_wq, wqrd), (wk, b_wk, wkrrd), (wv, b_wv, wvrd), (wg, b_wg, wgrd)):
                P.dma("pool", t[:].rearrange("p a b -> p (a b)"), d_[:, h * 1024:(h + 1) * 1024], [], [bb])
            act(lambda e, h=h: e.activation(out=dtmp[:, 0, :], in_=dposm[:, 0, :], func=AF.Exp, scale=lgf[:, h:h + 1]), [b_dec], [b_Dh])
            act(lambda e, h=h: e.activation(out=dtmp[:, 1, :], in_=dposm[:, 1, :], func=AF.Exp, scale=lgb[:, h:h + 1]), [b_dec], [b_Dh])
            dve(lambda e: e.tensor_tensor(out=dtmp[:, 0, :], in0=dtmp[:, 0, :], in1=dposm[:, 2, :], op=ALU.mult), [b_Dh, b_dec], [b_Dh])
            dve(lambda e: e.tensor_tensor(out=dtmp[:, 1, :], in0=dtmp[:, 1, :], in1=dposm[:, 3, :], op=ALU.mult), [b_Dh, b_dec], [b_Dh])
            dve(lambda e: e.tensor_tensor(out=Dh[:], in0=dtmp[:, 0, :], in1=dtmp[:, 1, :], op=ALU.add), [b_Dh], [b_Dh])
            act(lambda e, h=h: e.activation(out=qdF[:], in_=cp1[:], func=AF.Exp, scale=lgf[:, h:h + 1]), [b_dec], [b_qd])
            act(lambda e, h=h: e.activation(out=qdB[:], in_=cm[:], func=AF.Exp, scale=lgb[:, h:h + 1]), [b_dec], [b_qd])
            dve(lambda e: e.tensor_copy(out=scol[:, 0, :], in_=rstd_tm[:, 0:16]), [b_rtm[i] for i in range(NOB)], [b_scol])
            dve(lambda e: e.tensor_tensor(out=scol[:, 2, :], in0=rstd_tm[:, 0:16], in1=rstd_tm[:, 0:16], op=ALU.mult),
                [b_rtm[i] for i in range(NOB)], [b_scol])
            dve(lambda e, h=h: e.tensor_scalar(out=scol[:, 1, :], in0=scol[:, 2, :], scalar1=dkf[:, h:h + 1], scalar2=0.125,
                                               op0=ALU.mult, op1=ALU.mult), [b_scol, b_dec], [b_scol])
            dve(lambda e, h=h: e.tensor_scalar(out=scol[:, 2, :], in0=scol[:, 2, :], scalar1=dkb[:, h:h + 1], scalar2=0.125,
                                               op0=ALU.mult, op1=ALU.mult), [b_scol, b_dec], [b_scol])
            ps_g = ps_ret
            for t in range(16):
                b = t // 4
                cs = slice(t * 128, (t + 1) * 128)
                ps, b_ps = ps_g.get1()
                mm(ps[:, 0:128], [(xg_own[:, k, cs], wk[:, k, :]) for k in range(8)], [b_wk, b_xgown[b]], b_ps)
                mm(ps[:, 128:256], [(xg_own[:, k, cs], wv[:, k, :]) for k in range(8)], [b_wv, b_xgown[b]], b_ps)
                dve(lambda e, ps=ps, t=t: e.tensor_tensor(out=TTt[:, t, :], in0=ps[:, 0:128], in1=trigT[:, t, :], op=ALU.mult),
                    [b_ps, b_trigT[b]], [b_Vt[t]])
                act(lambda e, ps=ps, t=t: e.activation(out=Vt[:, t, :], in_=ps[:, 128:256], func=AF.Copy, scale=scol[:, 0, t:t + 1]),
                    [b_ps, b_scol], [b_Vt[t]])
                act(lambda e, ps=ps, t=t: e.activation(out=VFt[:, t, :], in_=ps[:, 128:256], func=AF.Copy, scale=scol[:, 1, t:t + 1]),
                    [b_ps, b_scol], [b_Vt[t]])
                act(lambda e, ps=ps, t=t: e.activation(out=VBt[:, t, :], in_=ps[:, 128:256], func=AF.Copy, scale=scol[:, 2, t:t + 1]),
                    [b_ps, b_scol], [b_Vt[t]])

            def scans(h=h):
                act(lambda e: e.copy(out=stF[0][:], in_=TF0[:, h, :]), [b_T0], [b_stF[0]])
                act(lambda e: e.copy(out=stB[0][:], in_=TB0[:, h, :]), [b_T0], [b_stB[0]])
                act(lambda e: e.copy(out=SFb[:, 0, :], in_=TF0[:, h, :]), [b_T0], [b_SF[0]])
                act(lambda e: e.copy(out=SBb[:, 15, :], in_=TB0[:, h, :]), [b_T0], [b_SB[15]])
                pus = {}

                def issue_u(s_):
                    pu, b_pu = ps_g.get1()
                    tf, tb = s_, 15 - s_
                    mm(pu[:, 0:128], [(TTt[:, tf, :], VFt[:, tf, :])], [b_Vt[tf]], b_pu)
                    mm(pu[:, 128:256], [(TTt[:, tb, :], VBt[:, tb, :])], [b_Vt[tb]], b_pu)
                    pus[s_] = (pu, b_pu)
                issue_u(0)
                cur = 0
                for s_ in range(15):
                    if s_ + 1 < 15:
                        issue_u(s_ + 1)
                    pu, b_pu = pus.pop(s_)
                    dve(lambda e, pu=pu, cur=cur: e.scalar_tensor_tensor(out=stF[1 - cur][:], in0=stF[cur][:], scalar=gCf[:, h:h + 1],
                                                                         in1=pu[:, 0:128], op0=ALU.mult, op1=ALU.add),
                        [b_pu, b_stF[cur], b_dec], [b_stF[1 - cur]])
                    dve(lambda e, pu=pu, cur=cur: e.scalar_tensor_tensor(out=stB[1 - cur][:], in0=stB[cur][:], scalar=gCb[:, h:h + 1],
                                                                         in1=pu[:, 128:256], op0=ALU.mult, op1=ALU.add),
                        [b_pu, b_stB[cur], b_dec], [b_stB[1 - cur]])
                    cur = 1 - cur
                    act(lambda e, s_=s_, cur=cur: e.copy(out=SFb[:, s_ + 1, :], in_=stF[cur][:]), [b_stF[cur]], [b_SF[s_ + 1]])
                    act(lambda e, s_=s_, cur=cur: e.copy(out=SBb[:, 14 - s_, :], in_=stB[cur][:]), [b_stB[cur]], [b_SB[14 - s_]])
                    yield
            sc_gen = scans()

            for b in range(NOB):
                c0 = b * TB
                ps, b_ps = ps_g.get1()
                mm(ps, [(wq[:, k, :], xg_own[:, k, c0:c0 + TB]) for k in range(8)], [b_wq, b_xgown[b]], b_ps)
                tq, b_tq = h_pool.get()
                dve(lambda e, tq=tq, ps=ps, c0=c0: e.tensor_tensor(out=tq[:], in0=ps, in1=trigR[:, c0:c0 + TB], op=ALU.mult),
                    [b_ps, b_trigR[b]], [b_tq])
                psk, b_psk = ps_g.get1()
                mm(psk, [(wk[:, k, :], xg_own[:, k, c0:c0 + TB]) for k in range(8)], [b_wk, b_xgown[b]], b_psk)
                tk, b_tk = h_pool.get()
                dve(lambda e, tk=tk, psk=psk, c0=c0: e.tensor_tensor(out=tk[:], in0=psk, in1=trigR[:, c0:c0 + TB], op=ALU.mult),
                    [b_psk, b_trigR[b]], [b_tk])
                next(sc_gen, None)
                ps2, b_ps2 = ps_g.get1()
                mm(ps2, [(Jt[:], tq[:])], [b_tq, b_const], b_ps2)
                act(lambda e, ps2=ps2, c0=c0: e.copy(out=QQ[:, c0:c0 + TB], in_=ps2), [b_ps2], [b_QQ[b]])
                dve(lambda e, ps2=ps2, c0=c0: e.tensor_tensor(out=QQF[:, c0:c0 + TB], in0=ps2, in1=qdF[:], op=ALU.mult), [b_ps2, b_qd], [b_QQ[b]])
                dve(lambda e, ps2=ps2, c0=c0: e.tensor_tensor(out=QQB[:, c0:c0 + TB], in0=ps2, in1=qdB[:], op=ALU.mult), [b_ps2, b_qd], [b_QQ[b]])
                ps3, b_ps3 = ps_g.get1()
                mm(ps3, [(Jt[:], tk[:])], [b_tk, b_const], b_ps3)
                act(lambda e, ps3=ps3, c0=c0: e.activation(out=KK[:, c0:c0 + TB], in_=ps3, func=AF.Copy, scale=0.125), [b_ps3], [b_KK[b]])
                next(sc_gen, None)
                psg, b_psg = ps_g.get1()
                mm(psg, [(wg[:, k, :], xg_own[:, k, c0:c0 + TB]) for k in range(8)], [b_wg, b_xgown[b]], b_psg)
                zt, b_zt = f_pool.get()
                dve(lambda e, zt=zt, psg=psg, c0=c0: e.tensor_tensor(out=zt[:], in0=psg, in1=rstdx_own[:, c0:c0 + TB], op=ALU.mult),
                    [b_psg, b_rstdx_own[b]], [b_zt])
                act(lambda e, zt=zt, c0=c0: e.activation(out=SG[:, c0:c0 + TB], in_=zt[:], func=AF.Silu), [b_zt], [b_SG[b]])
                next(sc_gen, None)
                next(sc_gen, None)
            for _ in sc_gen:
                pass

            pas = {}

            def issue_pa(t):
                cs = slice(t * 128, (t + 1) * 128)
                pa, b_pa = ps_g.get1()
                mm(pa[:, 0:128], [(KK[0:64, cs], QQ[0:64, cs])], [b_KK[t // 4], b_QQ[t // 4]], b_pa)
                pas[t] = (pa, b_pa)

            fin = {}

            def fin1(b):
                po, b_po = po_ret[b % 2]
                sqo, b_sqo = h_pool.get()
                act(lambda e, sqo=sqo, po=po: e.activation(out=sqo[:], in_=po, func=AF.Square), [b_po], [b_sqo])
                fin[b] = [sqo, b_sqo]

            def fin2(b):
                sqo, b_sqo = fin[b]
                ps2, b_ps2 = ps_g.get1()
                mm(ps2, [(ones_bf[:], sqo[:])], [b_sqo, b_const], b_ps2)
                ro, b_ro = f_pool.get()
                rstd(ro[:], ps2, 128.0, 0, [b_ps2], b_ro)
                fin[b] = [ro, b_ro]

            def fin3(b, h=h):
                po, b_po = po_ret[b % 2]
                ro, b_ro = fin.pop(b)
                c0 = b * TB
                dve(lambda e, ro=ro, po=po: e.tensor_tensor(out=ro[:], in0=po, in1=ro[:], op=ALU.mult), [b_po, b_ro], [b_ro])
                dve(lambda e, ro=ro, c0=c0: e.tensor_tensor(out=OB[:, h, c0:c0 + TB], in0=ro[:], in1=SG[:, c0:c0 + TB], op=ALU.mult),
                    [b_ro, b_SG[b]], [b_OB[b]])

            issue_pa(0)
            for t in range(16):
                b, j = t // 4, t % 4
                cs = slice(t * 128, (t + 1) * 128)
                if t + 1 < 16:
                    issue_pa(t + 1)
                po, b_po = po_ret[b % 2]
                pa, b_pa = pas.pop(t)
                ad, b_ad = ad_pool.get()
                dve(lambda e, ad=ad, pa=pa: e.tensor_tensor(out=ad[:], in0=pa[:, 0:128], in1=Dh[:], op=ALU.mult), [b_pa, b_Dh], [b_ad])
                mm(po[:, j * 128:(j + 1) * 128], [(Vt[:, t, :], ad[:]), (SFb[:, t, :], QQF[:, cs]), (SBb[:, t, :], QQB[:, cs])],
                   [b_Vt[t], b_ad, b_SF[t], b_SB[t], b_QQ[b]], b_po)
                if j == 3:
                    fin1(b)
                if b >= 1 and j == 1:
                    fin2(b - 1)
                if b >= 1 and j == 3:
                    fin3(b - 1)
            fin2(3)
            fin3(3)
        if stop == '2':
            P.barrier()
            P.emit()
            return nc
        P.at_kb(116)
        wmla = P.sb("wmla", [128, 4, 1024], BF16)
        wret = P.sb("wret", [128, 8, 1024], BF16)
        wout = P.sb("wout", [128, 8, 1024], BF16)
        b_w4 = Buf("w4")
        for (t, d_) in ((wmla, wmlad), (wret, wretd), (wout, woutd)):
            P.dma("pool", t[:].rearrange("p a b -> p (a b)"), d_, [], [b_w4])
        wgate3 = wgated.rearrange("p (k n) -> p k n", k=8)
        wg_pool = RPool(P, "wgt", [128, 8, 256], BF16, 4)
        xs_pool = RPool(P, "xs4", [128, TB], F32, 3)
        f_pool = RPool(P, "ftmp4", [128, TB], F32, 5)
        mg_pool = RPool(P, "mg", [128, 8, TB], BF16, 1)
        x1_pool = RPool(P, "x1t", [128, TB], F32, 3)
        sq4_pool = RPool(P, "sq4", [128, TB], BF16, 2)
        _top = P.top
        P.top = 16640 + 76 * 1024
        rstd2 = P.sb("rstd2", [128, OWN], F32)
        P.top = _top
        b_rstd2 = [Buf("rstd2_%d" % i) for i in range(NOB)]
        ps_p4 = PsPool(pairs[0:3])
        pss4, b_pss4 = pairs[3][0][:, 0:512], pairs[3][1]
        x1_writes = []
        for b in range(NOB):
            c0 = b * TB
            mg, b_mg = mg_pool.get()
            for c in range(8):
                cs = slice(c * 128, (c + 1) * 128)
                wgt, b_wgt = wg_pool.get()
                P.dma("pool", wgt[:, :, 0:128], wgate3[:, :, c * 128:(c + 1) * 128], [], [b_wgt])
                P.dma("pool", wgt[:, :, 128:256], wgate3[:, :, 1024 + c * 128:1024 + (c + 1) * 128], [], [b_wgt])
                pa, b_pa = ps_p4.get1()
                mm(pa, [(wmla[:, pr_, cs], OA[:, pr_, c0:c0 + TB]) for pr_ in range(4)], [b_w4, b_OA[b]], b_pa)
                pb, b_pb = ps_p4.get1()
                mm(pb, [(wret[:, k, cs], OB[:, k, c0:c0 + TB]) for k in range(8)], [b_w4, b_OB[b]], b_pb)
                g1, b_g1 = f_pool.get()
                g2, b_g2 = f_pool.get()
                for (gt, bg, off) in ((g1, b_g1, 0), (g2, b_g2, 128)):
                    pg, b_pg = ps_p4.get1()
                    mm(pg, [(wgt[:, k, off:off + 128], xg_own[:, k, c0:c0 + TB]) for k in range(8)], [b_wgt, b_xgown[b]], b_pg)
                    dve(lambda e, gt=gt, pg=pg, c0=c0: e.tensor_tensor(out=gt[:], in0=pg, in1=rstdx_own[:, c0:c0 + TB], op=ALU.mult),
                        [b_pg, b_rstdx_own[b]], [bg])
                    act(lambda e, gt=gt: e.activation(out=gt[:], in_=gt[:], func=AF.Sigmoid), [bg], [bg])
                dve(lambda e, g1=g1, pa=pa: e.tensor_tensor(out=g1[:], in0=pa, in1=g1[:], op=ALU.mult), [b_pa, b_g1], [b_g1])
                dve(lambda e, g2=g2, pb=pb: e.tensor_tensor(out=g2[:], in0=pb, in1=g2[:], op=ALU.mult), [b_pb, b_g2], [b_g2])
                dve(lambda e, g1=g1, g2=g2, mg=mg, c=c: e.tensor_tensor(out=mg[:, c, :], in0=g1[:], in1=g2[:], op=ALU.add),
                    [b_g1, b_g2], [b_mg])
            for c in range(8):
                cs = slice(c * 128, (c + 1) * 128)
                pw, b_pw = ps_p4.get1()
                mm(pw, [(wout[:, k, cs], mg[:, k, :]) for k in range(8)], [b_w4, b_mg], b_pw)
                xs, b_xs = xs_pool.get()
                P.dma("sp", xs[:], xT[c * 128:(c + 1) * 128, c0:c0 + TB], [], [b_xs])
                x1, b_x1 = x1_pool.get()
                dve(lambda e, x1=x1, pw=pw, xs=xs: e.tensor_tensor(out=x1[:], in0=pw, in1=xs[:], op=ALU.add), [b_pw, b_xs], [b_x1])
                x1_writes.append(P.dma("sp", x1s[c * 128:(c + 1) * 128, c0:c0 + TB], x1[:], [b_x1], []))
                sq4, b_sq4 = sq4_pool.get()
                act(lambda e, x1=x1, sq4=sq4: e.activation(out=sq4[:], in_=x1[:], func=AF.Square), [b_x1], [b_sq4])
                P.op("pe", lambda e, sq4=sq4, c=c: e.matmul(pss4, lhsT=ones_bf[:], rhs=sq4[:], start=(c == 0), stop=(c == 7)),
                     [b_sq4, b_const], [b_pss4])
            rstd(rstd2[:, c0:c0 + TB], pss4, float(D), 0, [b_pss4], b_rstd2[b])
        P.wait_ops("sp", x1_writes)

        if stop == '4':
            P.barrier()
            P.emit()
            return nc
        P.at_kb(20)
        x1n = P.sb("x1n", [128, 8, OWN], BF16)
        b_x1n = [Buf("x1n%d" % i) for i in range(NOB)]
        wgu_p = RPool(P, "wgu", [128, 8, 256], BF16, 3)
        wdn_p = RPool(P, "wdn", [128, 8, 128], BF16, 3)
        f_pool = RPool(P, "ftmp5", [128, TB], F32, 3)
        assert P.top <= 16640 + 76 * 1024, P.top
        P.top = 16640 + 84 * 1024
        actT = P.sb("actT", [128, 8, OWN], BF16)
        yp = P.sb("yp", [128, 8, OWN], F32)
        b_yp = [[Buf("yp%d_%d" % (c, i)) for i in range(NOB)] for c in range(8)]
        xs_pool = RPool(P, "xs5", [128, TB], F32, 6)
        wgu3 = wgud.rearrange("p (k n) -> p k n", k=8)
        wdn3 = wdownd.rearrange("p (c n) -> p c n", c=NFC)
        for b in range(NOB):
            c0 = b * TB
            for c in range(8):
                xs, b_xs = xs_pool.get()
                P.dma("sp", xs[:], x1s[c * 128:(c + 1) * 128, c0:c0 + TB], [], [b_xs])
                dve(lambda e, xs=xs, c=c, c0=c0: e.scalar_tensor_tensor(out=x1n[:, c, c0:c0 + TB], in0=xs[:], scalar=gffn[:, c:c + 1],
                                                                        in1=rstd2[:, c0:c0 + TB], op0=ALU.mult, op1=ALU.mult),
                    [b_xs, b_const, b_rstd2[b]], [b_x1n[b]])
        out_writes = []
        groups = [(0, 8), (8, 15), (15, 22)]
        for g, (f0, f1) in enumerate(groups):
            ng = f1 - f0
            b_act = [Buf("act%d_%d" % (g, i)) for i in range(NOB)]
            for ci in range(ng):
                fc = f0 + ci
                w, b_w = wgu_p.get()
                P.dma("pool", w[:, :, 0:128], wgu3[:, :, fc * 128:(fc + 1) * 128], [], [b_w])
                P.dma("pool", w[:, :, 128:256], wgu3[:, :, FH + fc * 128:FH + (fc + 1) * 128], [], [b_w])
                for b in range(NOB):
                    c0 = b * TB
                    pg, b_pg = ps_all.get1()
                    mm(pg, [(w[:, k, 0:128], x1n[:, k, c0:c0 + TB]) for k in range(8)], [b_w, b_x1n[b]], b_pg)
                    pu, b_pu = ps_all.get1()
                    mm(pu, [(w[:, k, 128:256], x1n[:, k, c0:c0 + TB]) for k in range(8)], [b_w, b_x1n[b]], b_pu)
                    gt, b_gt = f_pool.get()
                    act(lambda e, gt=gt, pg=pg: e.activation(out=gt[:], in_=pg, func=AF.Silu), [b_pg], [b_gt])
                    dve(lambda e, gt=gt, pu=pu, ci=ci, c0=c0: e.tensor_tensor(out=actT[:, ci, c0:c0 + TB], in0=pu, in1=gt[:], op=ALU.mult),
                        [b_pu, b_gt], [b_act[b]])
            for c in range(8):
                wd, b_wd = wdn_p.get()
                P.dma("pool", wd[:, 0:ng, :], wdn3[:, f0:f1, c * 128:(c + 1) * 128], [], [b_wd])
                for b in range(NOB):
                    c0 = b * TB
                    py, b_py = ps_all.get1()
                    mm(py, [(wd[:, ci, :], actT[:, ci, c0:c0 + TB]) for ci in range(ng)], [b_wd, b_act[b]], b_py)
                    if g == 0:
                        xs, b_xs = xs_pool.get()
                        P.dma("sp", xs[:], x1s[c * 128:(c + 1) * 128, c0:c0 + TB], [], [b_xs])
                        dve(lambda e, py=py, xs=xs, c=c, c0=c0: e.tensor_tensor(out=yp[:, c, c0:c0 + TB], in0=py, in1=xs[:], op=ALU.add),
                            [b_py, b_xs], [b_yp[c][b]])
                    elif g == 1:
                        dve(lambda e, py=py, c=c, c0=c0: e.tensor_tensor(out=yp[:, c, c0:c0 + TB], in0=py, in1=yp[:, c, c0:c0 + TB], op=ALU.add),
                            [b_py, b_yp[c][b]], [b_yp[c][b]])
                    else:
                        ot, b_ot = f_pool.get()
                        dve(lambda e, py=py, ot=ot, c=c, c0=c0: e.tensor_tensor(out=ot[:], in0=py, in1=yp[:, c, c0:c0 + TB], op=ALU.add),
                            [b_py, b_yp[c][b]], [b_ot])
                        out_writes.append(P.dma("sp", yT[c * 128:(c + 1) * 128, c0:c0 + TB], ot[:], [b_ot], []))
        P.wait_ops("sp", out_writes)
        P.emit()
    return nc


def _kchunks(w):
    K, N = w.shape
    return np.ascontiguousarray(w.reshape(K // 128, 128, N).transpose(1, 0, 2)).reshape(128, -1)


def _prep_weights(inp):
    f = lambda a: np.asarray(a, dtype=np.float32)
    w_in = f(inp["w_in"])
    out = {}
    out["gmix"] = np.ascontiguousarray(f(inp["g_mix"]).reshape(8, 128).T)
    out["gffn"] = np.ascontiguousarray(f(inp["g_ffn"]).reshape(8, 128).T)
    out["gqa"] = np.ascontiguousarray(f(inp["g_q_a"]).reshape(2, 128).T)
    out["gkva"] = np.ascontiguousarray(f(inp["g_kv_a"]).reshape(1, 128).T)

    def gaug(g):
        g = f(g)
        return np.concatenate([g[64:96], g[80:96], g[64:80], g[0:64]]).reshape(128, 1).copy()
    out["gqaug"] = gaug(inp["g_qn"])
    out["gkaug"] = gaug(inp["g_kn"])
    out["decf"] = f(inp["ret_decay_fwd"]).copy()
    out["decb"] = f(inp["ret_decay_bwd"]).copy()
    out["wcq"] = _kchunks(w_in[:, 0:256])
    out["wckv"] = _kchunks(w_in[:, 256:384])
    kr = w_in[:, 384:416]
    out["wkr"] = _kchunks(np.concatenate([kr, kr[:, 16:32], kr[:, 0:16]], axis=1))

    def aug_r(w):
        w = w.reshape(1024, 8, 64)
        return np.concatenate([w, w[:, :, 32:64], w[:, :, 0:32]], axis=2)
    qa = aug_r(w_in[:, 416:928])
    ka = aug_r(w_in[:, 928:1440])
    vr = w_in[:, 1440:2464].reshape(1024, 8, 128)
    gr = w_in[:, 2464:3488].reshape(1024, 8, 128)

    def per_head(w):
        return np.ascontiguousarray(w.reshape(8, 128, 8, 128).transpose(1, 2, 0, 3)).reshape(128, -1)
    out["wqr"] = per_head(qa)
    out["wkrr"] = per_head(ka)
    out["wvr"] = per_head(vr)
    out["wgr"] = per_head(gr)
    out["wkall"] = _kchunks(ka.reshape(1024, 1024))
    out["wvall"] = _kchunks(vr.reshape(1024, 1024))
    out["wgate"] = _kchunks(w_in[:, 3488:5536])
    wqb = f(inp["w_q_b"]).reshape(256, 8, 96)
    wqb_aug = np.concatenate([wqb[:, :, 64:96], wqb[:, :, 80:96], wqb[:, :, 64:80], wqb[:, :, 0:64]], axis=2)
    out["wqb"] = _kchunks(wqb_aug.reshape(256, 1024))
    wkvb = f(inp["w_kv_b"]).reshape(128, 8, 128)
    out["wkvbk"] = np.ascontiguousarray(wkvb[:, :, 0:64]).reshape(128, -1)
    out["wkvbv"] = np.ascontiguousarray(wkvb[:, :, 64:128]).reshape(128, -1)
    out["wmla"] = _kchunks(f(inp["w_mla_out"]))
    out["wret"] = _kchunks(f(inp["w_ret_out"]))
    out["wout"] = _kchunks(f(inp["w_out"]))
    out["wgu"] = _kchunks(f(inp["w_gate_up"]))
    out["wdown"] = _kchunks(f(inp["w_down"]))
    return out


_NC_CACHE = {}


def kernel(**inputs):
    x = np.asarray(inputs["x"], dtype=np.float32)
    positions = np.asarray(inputs["positions"]).astype(np.int32)
    wts = _prep_weights(inputs)
    in_maps = []
    for c in range(8):
        b, j = c // 4, c % 4
        order = np.concatenate([np.arange(j * OWN, (j + 1) * OWN),
                                np.arange(0, j * OWN), np.arange((j + 1) * OWN, NT)])
        m = dict(wts)
        m["xT"] = np.ascontiguousarray(x[b][order].T)
        m["pos"] = np.ascontiguousarray(positions[b][order].reshape(1, NT))
        rel = (order - j * OWN).astype(np.float32)
        m["rel"] = np.ascontiguousarray(rel.reshape(64, 128).T)
        in_maps.append(m)
    if "nc" not in _NC_CACHE:
        _NC_CACHE["nc"] = build_nc()
    nc = _NC_CACHE["nc"]
    res = run_bass_kernel_spmd(nc, in_maps, core_ids=list(range(8)))
    out = np.empty((2, NT, D), dtype=np.float32)
    for c in range(8):
        b, j = c // 4, c % 4
        out[b, j * OWN:(j + 1) * OWN, :] = np.asarray(res.results[c]["yT"], dtype=np.float32).T
    return out
```

```python
import numpy as np
import concourse.bass as bass
import concourse.mybir as mybir
from concourse.bass_utils import run_bass_kernel_spmd
from contextlib import ExitStack

F32 = mybir.dt.float32
BF16 = mybir.dt.bfloat16
I32 = mybir.dt.int32
AF = mybir.ActivationFunctionType
ALU = mybir.AluOpType

ENGS = ("pe", "act", "dve", "pool", "sp")
NDMASEM = 24
PI = float(np.pi)
EPS = 1e-6
THETA = 10000.0

D = 1024
NT = 8192
OWN = 2048
TB = 512
NB = NT // TB
NOB = OWN // TB
FH = 2816
NFC = FH // 128


class Buf:
    __slots__ = ("name", "w", "r", "excl")

    def __init__(self, name="", excl=False):
        self.name = name
        self.w = None
        self.r = []
        self.excl = excl


class Op:
    __slots__ = ("eng", "fn", "deps", "signal", "sem", "val", "is_dma")

    def __init__(self, eng, fn, is_dma=False):
        self.eng = eng
        self.fn = fn
        self.deps = []
        self.signal = False
        self.sem = None
        self.val = 0
        self.is_dma = is_dma


class Prog:
    def __init__(self, nc):
        self.nc = nc
        self.ops = {e: [] for e in ENGS}
        self.stack = ExitStack()
        self.dma_pending = []
        self.top = 16640
        self.limit = 229376
        self.nalloc = 0

    def sb(self, name, shape, dt):
        n = 1
        for s in shape[1:]:
            n *= s
        nbytes = n * (4 if dt in (F32, I32) else 2)
        nbytes = (nbytes + 63) // 64 * 64
        off = self.top
        self.top += nbytes
        assert self.top <= self.limit, "SBUF overflow at %s: %d" % (name, self.top)
        self.nalloc += 1
        return self.nc.alloc_sbuf_tensor_at("%s_%d" % (name, self.nalloc), list(shape), dt, offset=off)

    def at_kb(self, kb):
        self.barrier()
        self.top = 16640 + int(kb * 1024)

    def ps(self, name, shape, dt=F32):
        return self.stack.enter_context(self.nc.psum_tensor(name, list(shape), dt))

    def op(self, eng, fn, reads=(), writes=(), dma=False):
        o = Op(eng, fn, dma)
        deps = []
        xr = [b for b in reads if b.excl]
        if xr:
            reads = [b for b in reads if not b.excl]
            writes = list(writes) + [b for b in xr if b not in writes]
        for b in reads:
            if b.w is not None:
                deps.append(b.w)
        for b in writes:
            if b.w is not None:
                deps.append(b.w)
            deps.extend(b.r)
        for b in reads:
            b.r.append(o)
        for b in writes:
            b.w = o
            b.r = []
        seen = set()
        for d in deps:
            if d is o or id(d) in seen:
                continue
            seen.add(id(d))
            if d.eng == "pe" and eng == "pe" and not d.is_dma and not dma:
                continue
            o.deps.append(d)
            d.signal = True
        self.ops[eng].append(o)
        return o

    def dma(self, eng, out, in_, reads=(), writes=()):
        o = self.op(eng, lambda e: e.dma_start(out=out, in_=in_), reads, writes, dma=True)
        self.dma_pending.append(o)
        return o

    def barrier(self):
        last = [self.ops[e][-1] for e in ENGS if self.ops[e] and not self.ops[e][-1].is_dma
                and self.ops[e][-1].fn is not None]
        dmas = self.dma_pending
        self.dma_pending = []
        for e in ENGS:
            o = Op(e, None)
            for d in last + dmas:
                if d.eng == e and not d.is_dma:
                    continue
                o.deps.append(d)
                d.signal = True
            self.ops[e].append(o)

    def wait_ops(self, eng, ops):
        o = Op(eng, None)
        for d in ops:
            o.deps.append(d)
            d.signal = True
        self.ops[eng].append(o)
        return o

    def emit(self):
        nc = self.nc
        st = self.stack
        sems = {e: st.enter_context(nc.semaphore("s_" + e)) for e in ENGS}
        dsems = {e: [st.enter_context(nc.semaphore("d_%s_%d" % (e, i))) for i in range(NDMASEM)]
                 for e in ("sp", "pool", "act")}
        for e in ENGS:
            cnt = 0
            nd = 0
            for o in self.ops[e]:
                if o.is_dma:
                    o.sem = dsems[e][nd % NDMASEM]
                    o.val = 16 * (nd // NDMASEM + 1)
                    nd += 1
                elif o.signal:
                    cnt += 1
                    o.sem = sems[e]
                    o.val = cnt
        prog = self

        def run_engine(ename, eng):
            known = {}
            for o in prog.ops[ename]:
                waits = [(d.sem, d.val) for d in o.deps]
                if o.is_dma and o.val > 16:
                    waits.append((o.sem, o.val - 16))
                for (s, v) in waits:
                    key = id(s)
                    if known.get(key, 0) >= v:
                        continue
                    eng.wait_ge(s, v)
                    known[key] = v
                if o.fn is None:
                    continue
                ins = o.fn(eng)
                if o.is_dma:
                    ins.then_inc(o.sem, 16)
                elif o.signal:
                    ins.then_inc(o.sem, 1)

        with nc.Block() as block:
            @block.tensor
            def _(eng):
                run_engine("pe", eng)

            @block.scalar
            def _(eng):
                run_engine("act", eng)

            @block.vector
            def _(eng):
                run_engine("dve", eng)

            @block.gpsimd
            def _(eng):
                run_engine("pool", eng)

            @block.sync
            def _(eng):
                run_engine("sp", eng)


class RPool:
    def __init__(self, P, name, shape, dt, n):
        self.items = [(P.sb("%s%d" % (name, i), shape, dt), Buf("%s%d" % (name, i))) for i in range(n)]
        self.i = 0

    def get(self):
        it = self.items[self.i % len(self.items)]
        self.i += 1
        return it


class PsPool:
    def __init__(self, pairs):
        self.pairs = pairs
        self.i1 = 0
        self.i2 = 0

    def get1(self):
        n = len(self.pairs) * 2
        k = self.i1 % n
        self.i1 += 1
        t, b0, b1 = self.pairs[k // 2]
        return (t[:, 0:512], b0) if k % 2 == 0 else (t[:, 512:1024], b1)

    def get2(self):
        t, b0, b1 = self.pairs[self.i2 % len(self.pairs)]
        self.i2 += 1
        return t, b0, b1


def build_nc(debug=False, stop=None):
    nc = bass.Bass("TRN2", target_bir_lowering=False)

    def din(name, shape, dt=F32):
        return nc.dram_tensor(name, list(shape), dt, kind="ExternalInput").ap()

    xT = din("xT", [D, NT])
    posd = din("pos", [1, NT], I32)
    reld = din("rel", [128, 64])
    gmixd = din("gmix", [128, 8])
    gffnd = din("gffn", [128, 8])
    gqad = din("gqa", [128, 2])
    gkvad = din("gkva", [128, 1])
    gqaugd = din("gqaug", [128, 1])
    gkaugd = din("gkaug", [128, 1])
    decfd = din("decf", [8])
    decbd = din("decb", [8])
    wcqd = din("wcq", [128, 8 * 256])
    wckvd = din("wckv", [128, 8 * 128])
    wkrd = din("wkr", [128, 8 * 64])
    wqrd = din("wqr", [128, 8 * 1024])
    wkrrd = din("wkrr", [128, 8 * 1024])
    wvrd = din("wvr", [128, 8 * 1024])
    wgrd = din("wgr", [128, 8 * 1024])
    wkalld = din("wkall", [128, 8 * 1024])
    wvalld = din("wvall", [128, 8 * 1024])
    wgated = din("wgate", [128, 8 * 2048])
    wqbd = din("wqb", [128, 2 * 1024])
    wkvbkd = din("wkvbk", [128, 8 * 64])
    wkvbvd = din("wkvbv", [128, 8 * 64])
    wmlad = din("wmla", [128, 4 * 1024])
    wretd = din("wret", [128, 8 * 1024])
    woutd = din("wout", [128, 8 * 1024])
    wgud = din("wgu", [128, 8 * 2 * FH])
    wdownd = din("wdown", [128, NFC * 1024])
    yT = nc.dram_tensor("yT", [D, OWN], F32, kind="ExternalOutput").ap()
    x1s = nc.dram_tensor("x1s", [D, OWN], F32, kind="Internal").ap()

    P = Prog(nc)
    with P.stack:
        pairs = []
        for i in range(4):
            t = P.ps("psum%d" % i, [128, 1024], F32)
            pairs.append((t, Buf("ps%da" % i, True), Buf("ps%db" % i, True)))
        ps_all = PsPool(pairs)
        ps_lo = PsPool(pairs[0:2])

        def act(fn, reads, writes):
            return P.op("act", fn, reads, writes)

        def dve(fn, reads, writes):
            return P.op("dve", fn, reads, writes)

        def pool(fn, reads, writes):
            return P.op("pool", fn, reads, writes)

        def mm(out, pairs_, reads, wbuf, mid=None):
            n = len(pairs_)
            for i, (l, r) in enumerate(pairs_):
                P.op("pe", lambda e, l=l, r=r, i=i: e.matmul(out, lhsT=l, rhs=r, start=(i == 0), stop=(i == n - 1)),
                     reads, [wbuf])
                if mid is not None and i == n // 2 - 1:
                    mid()

        def rstd(out, ps_in, n, eps, reads, wbuf):
            act(lambda e: e.activation(out=out, in_=ps_in, func=AF.Ln, scale=1.0 / n, bias=eps_t[0:ps_in.shape[0], eps:eps + 1]),
                list(reads) + [b_eps], [wbuf])
            act(lambda e: e.activation(out=out, in_=out, func=AF.Exp, scale=-0.5), [wbuf], [wbuf])

        b_const = Buf("const")
        b_eps = Buf("eps")
        eps_t = P.sb("eps_t", [128, 2], F32)
        dve(lambda e: e.memset(eps_t[:, 0:1], EPS), [], [b_eps])
        dve(lambda e: e.memset(eps_t[:, 1:2], 0.0), [], [b_eps])
        ones_bf = P.sb("ones_bf", [128, 128], BF16)
        maskq = P.sb("maskq", [128, 128], BF16)
        Jt = P.sb("Jt", [128, 128], BF16)
        JMt = P.sb("JMt", [128, 128], BF16)
        cf = P.sb("cf", [128, 128], F32)
        c1 = P.sb("c1", [128, 384], F32)
        dve(lambda e: e.memset(ones_bf[:], 1.0), [], [b_const])
        dve(lambda e: e.memset(maskq[:], 1.0), [], [b_const])
        dve(lambda e: e.memset(maskq[32:64, :], 0.0), [], [b_const])
        b_cf = Buf("cf")
        pool(lambda e: e.iota(cf[:], [[1, 128]], base=64, channel_multiplier=-1, allow_small_or_imprecise_dtypes=True),
             [], [b_cf])
        dve(lambda e: e.tensor_scalar(out=c1[:, 0:128], in0=cf[:], scalar1=64.0, scalar2=None, op0=ALU.is_equal), [b_cf], [b_cf])
        dve(lambda e: e.tensor_scalar(out=c1[:, 128:256], in0=cf[:], scalar1=0.0, scalar2=None, op0=ALU.is_equal), [b_cf], [b_cf])
        dve(lambda e: e.tensor_scalar(out=c1[:, 256:384], in0=cf[:], scalar1=128.0, scalar2=None, op0=ALU.is_equal), [b_cf], [b_cf])
        dve(lambda e: e.tensor_tensor(out=cf[:], in0=c1[:, 0:128], in1=c1[:, 128:256], op=ALU.add), [b_cf], [b_cf])
        dve(lambda e: e.tensor_tensor(out=cf[:], in0=cf[:], in1=c1[:, 256:384], op=ALU.add), [b_cf], [b_cf])
        dve(lambda e: e.tensor_copy(out=Jt[:], in_=cf[:]), [b_cf], [b_const])
        dve(lambda e: e.tensor_scalar(out=c1[:, 128:256], in0=c1[:, 128:256], scalar1=0.0, scalar2=None, op0=ALU.mult), [b_cf], [b_cf])
        pool(lambda e: e.iota(cf[:], [[1, 128]], base=64, channel_multiplier=-1, allow_small_or_imprecise_dtypes=True),
             [b_cf], [b_cf])
        dve(lambda e: e.tensor_scalar(out=c1[:, 256:288], in0=cf[:, 0:32], scalar1=32.0, scalar2=None, op0=ALU.is_equal), [b_cf], [b_cf])
        dve(lambda e: e.tensor_tensor(out=c1[:, 128:160], in0=c1[:, 0:32], in1=c1[:, 256:288], op=ALU.add), [b_cf], [b_cf])
        dve(lambda e: e.tensor_copy(out=c1[:, 192:256], in_=c1[:, 64:128]), [b_cf], [b_cf])
        dve(lambda e: e.tensor_copy(out=JMt[:], in_=c1[:, 128:256]), [b_cf], [b_const])

        gmix = P.sb("gmix", [128, 8], F32)
        gffn = P.sb("gffn", [128, 8], F32)
        gqa = P.sb("gqa", [128, 2], F32)
        gkva = P.sb("gkva", [128, 1], F32)
        gqaug = P.sb("gqaug", [128, 1], F32)
        gkaug = P.sb("gkaug", [128, 1], F32)
        for (t, d_) in ((gmix, gmixd), (gffn, gffnd), (gqa, gqad), (gkva, gkvad), (gqaug, gqaugd), (gkaug, gkaugd)):
            P.dma("sp", t[:], d_, [], [b_const])
        lgf = P.sb("lgf", [128, 8], F32)
        lgb = P.sb("lgb", [128, 8], F32)
        gCf = P.sb("gCf", [128, 8], F32)
        gCb = P.sb("gCb", [128, 8], F32)
        dkf = P.sb("dkf", [128, 8], F32)
        dkb = P.sb("dkb", [128, 8], F32)
        pidx = P.sb("pidx", [128, 2], F32)
        b_dec = Buf("dec")
        P.dma("sp", lgf[:], decfd.partition_broadcast(128), [], [b_dec])
        P.dma("sp", lgb[:], decbd.partition_broadcast(128), [], [b_dec])
        pool(lambda e: e.iota(pidx[:, 0:1], [[0, 1]], base=0, channel_multiplier=1, allow_small_or_imprecise_dtypes=True), [], [b_dec])
        pool(lambda e: e.iota(pidx[:, 1:2], [[0, 1]], base=127, channel_multiplier=-1, allow_small_or_imprecise_dtypes=True), [], [b_dec])
        for t in (lgf, lgb):
            act(lambda e, t=t: e.activation(out=t[:], in_=t[:], func=AF.Exp), [b_dec], [b_dec])
            dve(lambda e, t=t: e.tensor_scalar(out=t[:], in0=t[:], scalar1=-1.0, scalar2=None, op0=ALU.mult), [b_dec], [b_dec])
        act(lambda e: e.activation(out=gCf[:], in_=lgf[:], func=AF.Exp, scale=128.0), [b_dec], [b_dec])
        act(lambda e: e.activation(out=gCb[:], in_=lgb[:], func=AF.Exp, scale=128.0), [b_dec], [b_dec])
        act(lambda e: e.activation(out=dkf[:], in_=lgf[:], func=AF.Exp, scale=pidx[:, 1:2]), [b_dec], [b_dec])
        act(lambda e: e.activation(out=dkb[:], in_=lgb[:], func=AF.Exp, scale=pidx[:, 0:1]), [b_dec], [b_dec])

        diff = P.sb("diff", [128, 128], F32)
        dposm = P.sb("dposm", [128, 4, 128], F32)
        cp1 = P.sb("cp1", [128, 512], F32)
        cm = P.sb("cm", [128, 512], F32)
        pool(lambda e: e.iota(diff[:], [[1, 128]], base=0, channel_multiplier=-1, allow_small_or_imprecise_dtypes=True), [], [b_dec])
        pool(lambda e: e.iota(cp1[:].rearrange("p (a c) -> p a c", a=4), [[0, 4], [1, 128]], base=1, channel_multiplier=0,
                              allow_small_or_imprecise_dtypes=True), [], [b_dec])
        pool(lambda e: e.iota(cm[:].rearrange("p (a c) -> p a c", a=4), [[0, 4], [-1, 128]], base=128, channel_multiplier=0,
                              allow_small_or_imprecise_dtypes=True), [], [b_dec])
        dve(lambda e: e.tensor_scalar(out=dposm[:, 0, :], in0=diff[:], scalar1=0.0, scalar2=None, op0=ALU.max), [b_dec], [b_dec])
        dve(lambda e: e.tensor_scalar(out=dposm[:, 1, :], in0=diff[:], scalar1=-1.0, scalar2=0.0, op0=ALU.mult, op1=ALU.max), [b_dec], [b_dec])
        dve(lambda e: e.tensor_scalar(out=dposm[:, 2, :], in0=diff[:], scalar1=0.0, scalar2=None, op0=ALU.is_ge), [b_dec], [b_dec])
        dve(lambda e: e.tensor_scalar(out=dposm[:, 3, :], in0=diff[:], scalar1=0.0, scalar2=None, op0=ALU.is_lt), [b_dec], [b_dec])

        c2R = P.sb("c2R", [2, 128], F32)
        c2M = P.sb("c2M", [2, 128], F32)
        rw = P.sb("rw", [1, 512], F32)
        b_rw = Buf("rw")
        b_c2 = Buf("c2")
        pool(lambda e: e.iota(rw[:, 0:128].rearrange("p (a c) -> p a c", a=4), [[0, 4], [1, 32]], base=0, channel_multiplier=0,
                              allow_small_or_imprecise_dtypes=True), [], [b_rw])
        act(lambda e: e.activation(out=rw[:, 0:128], in_=rw[:, 0:128], func=AF.Exp, scale=-float(np.log(THETA)) / 32.0), [b_rw], [b_rw])
        dve(lambda e: e.memset(rw[:, 128:192], PI / 2), [], [b_rw])
        dve(lambda e: e.memset(rw[:, 192:224], PI), [], [b_rw])
        dve(lambda e: e.memset(rw[:, 224:256], 0.0), [], [b_rw])
        pool(lambda e: e.iota(rw[:, 256:320].rearrange("p (a c) -> p a c", a=4), [[0, 4], [1, 16]], base=0, channel_multiplier=0,
                              allow_small_or_imprecise_dtypes=True), [], [b_rw])
        act(lambda e: e.activation(out=rw[:, 256:320], in_=rw[:, 256:320], func=AF.Exp, scale=-float(np.log(THETA)) / 16.0), [b_rw], [b_rw])
        dve(lambda e: e.memset(rw[:, 320:384], 0.0), [], [b_rw])
        dve(lambda e: e.memset(rw[:, 384:416], PI / 2), [], [b_rw])
        dve(lambda e: e.memset(rw[:, 416:432], PI), [], [b_rw])
        dve(lambda e: e.memset(rw[:, 432:448], 0.0), [], [b_rw])
        dve(lambda e: e.memset(rw[:, 448:512], PI / 2), [], [b_rw])
        P.dma("sp", c2R[0:1, :], rw[:, 0:128], [b_rw], [b_c2])
        P.dma("sp", c2R[1:2, :], rw[:, 128:256], [b_rw], [b_c2])
        P.dma("sp", c2M[0:1, :], rw[:, 256:384], [b_rw], [b_c2])
        P.dma("sp", c2M[1:2, :], rw[:, 384:512], [b_rw], [b_c2])

        relt = P.sb("relt", [128, 64], F32)
        wfall = P.sb("wfall", [128, 8, 64], F32)
        wball = P.sb("wball", [128, 8, 64], F32)
        mfb = P.sb("mfb", [128, 2, 64], F32)
        etmp = P.sb("etmp", [128, 2, 64], F32)
        b_rel = Buf("rel")
        P.dma("sp", relt[:], reld, [], [b_rel])
        dve(lambda e: e.tensor_scalar(out=etmp[:, 0, :], in0=relt[:], scalar1=-1.0, scalar2=-1.0, op0=ALU.mult, op1=ALU.add), [b_rel], [b_rel])
        dve(lambda e: e.tensor_scalar(out=etmp[:, 0, :], in0=etmp[:, 0, :], scalar1=0.0, scalar2=None, op0=ALU.max), [b_rel], [b_rel])
        dve(lambda e: e.tensor_scalar(out=etmp[:, 1, :], in0=relt[:], scalar1=-float(OWN), scalar2=0.0, op0=ALU.add, op1=ALU.max), [b_rel], [b_rel])
        dve(lambda e: e.tensor_scalar(out=mfb[:, 0, :], in0=relt[:], scalar1=0.0, scalar2=0.125, op0=ALU.is_lt, op1=ALU.mult), [b_rel], [b_rel])
        dve(lambda e: e.tensor_scalar(out=mfb[:, 1, :], in0=relt[:], scalar1=float(OWN), scalar2=0.125, op0=ALU.is_ge, op1=ALU.mult), [b_rel], [b_rel])
        for h in range(8):
            act(lambda e, h=h: e.activation(out=wfall[:, h, :], in_=etmp[:, 0, :], func=AF.Exp, scale=lgf[:, h:h + 1]), [b_rel, b_dec], [b_rel])
            act(lambda e, h=h: e.activation(out=wball[:, h, :], in_=etmp[:, 1, :], func=AF.Exp, scale=lgb[:, h:h + 1]), [b_rel, b_dec], [b_rel])

        rstd_tm = P.sb("rstd_tm", [128, 64], F32)
        b_rtm = [Buf("rtm%d" % i) for i in range(NB)]
        assert P.top <= 16640 + 20 * 1024, P.top
        P.top = 16640 + 20 * 1024
        xg_own = P.sb("xg_own", [128, 8, OWN], BF16)
        b_xgown = [Buf("xgown%d" % i) for i in range(NOB)]
        rstdx_own = P.sb("rstdx_own", [128, OWN], F32)
        b_rstdx_own = [Buf("rstdxown%d" % i) for i in range(NOB)]
        OA = P.sb("OA", [128, 4, OWN], BF16)
        b_OA = [Buf("OA%d" % i) for i in range(NOB)]
        TF0 = P.sb("TF0", [128, 8, 128], F32)
        TB0 = P.sb("TB0", [128, 8, 128], F32)
        b_T0 = Buf("T0")
        ckvn = P.sb("ckvn", [128, NT], BF16)
        KRS = P.sb("KRS", [128, NT], BF16)
        cqn = P.sb("cqn", [128, 2, OWN], BF16)
        trigM_own = P.sb("trigM_own", [128, OWN], BF16)
        b_ckvn = [Buf("ckvn%d" % i) for i in range(NB)]
        b_KRS = [Buf("KRS%d" % i) for i in range(NB)]
        b_cqn = [Buf("cqn%d" % i) for i in range(NOB)]
        b_trigMo = [Buf("trigMo%d" % i) for i in range(NOB)]
        assert P.top == 16640 + 128 * 1024, P.top

        if stop == 'c':
            P.barrier()
            P.emit()
            return nc
        P.top = 16640 + 60 * 1024
        wkall = P.sb("wkall", [128, 8, 1024], BF16)
        P.top = 16640 + 128 * 1024
        wcq = P.sb("wcq", [128, 8, 256], BF16)
        wckv = P.sb("wckv", [128, 8, 128], BF16)
        wkr = P.sb("wkr", [128, 8, 64], BF16)
        wvall = P.sb("wvall", [128, 8, 1024], BF16)
        b_wcq, b_wckv, b_wkr, b_wkall, b_wvall = Buf("wcq"), Buf("wckv"), Buf("wkr"), Buf("wkall"), Buf("wvall")
        for (t, d_, bb) in ((wckv, wckvd, b_wckv), (wkr, wkrd, b_wkr), (wcq, wcqd, b_wcq), (wkall, wkalld, b_wkall), (wvall, wvalld, b_wvall)):
            P.dma("pool", t[:].rearrange("p a b -> p (a b)"), d_, [], [bb])

        xs_pool = RPool(P, "xs", [128, TB], F32, 3)
        sq_pool = RPool(P, "sq", [128, 8, TB], BF16, 1)
        xgb_pool = RPool(P, "xgb", [128, 8, TB], BF16, 2)
        f_pool = RPool(P, "ftmp", [128, TB], F32, 3)
        h_pool = RPool(P, "htmp", [128, TB], BF16, 3)
        rx_pool = RPool(P, "rx", [128, TB], F32, 1)
        trg_pool = RPool(P, "trg", [128, TB], BF16, 1)
        pr_pool = RPool(P, "posr", [2, TB], F32, 1)
        pi_pool = RPool(P, "posi", [1, TB], I32, 1)
        it_pool = RPool(P, "itmp", [128, TB], I32, 1)
        tt_pool = RPool(P, "ttok", [128, 1024], BF16, 1)
        vf_pool = RPool(P, "vf", [128, 1024], BF16, 1)
        vb_pool = RPool(P, "vb", [128, 1024], BF16, 1)
        trtb_pool = RPool(P, "trtb", [128, 4, 128], BF16, 2)
        sfb_pool = RPool(P, "sfb", [128, 2, 8, 4], F32, 2)
        bsb_pool = RPool(P, "bsb", [128, 2, 4], F32, 2)
        TFp, bTFa, bTFb = pairs[2]
        TBp, bTBa, bTBb = pairs[3]

        def trig_from_angle(out_ap, ang_ps, npart, reads, wbuf, width):
            it, b_it = it_pool.get()
            ft, b_ft = f_pool.get()
            dve(lambda e: e.tensor_scalar(out=it[0:npart, 0:width], in0=ang_ps, scalar1=1.0 / (2 * PI), scalar2=None, op0=ALU.mult),
                reads, [b_it])
            dve(lambda e: e.tensor_copy(out=ft[0:npart, 0:width], in_=it[0:npart, 0:width]), [b_it], [b_ft])
            dve(lambda e: e.scalar_tensor_tensor(out=ft[0:npart, 0:width], in0=ft[0:npart, 0:width], scalar=-2 * PI, in1=ang_ps,
                                                 op0=ALU.mult, op1=ALU.add), list(reads) + [b_ft], [b_ft])
            dve(lambda e: e.tensor_scalar(out=ft[0:npart, 0:width], in0=ft[0:npart, 0:width], scalar1=-3.1415925, scalar2=3.1415925,
                                          op0=ALU.max, op1=ALU.min), [b_ft], [b_ft])
            act(lambda e: e.activation(out=out_ap, in_=ft[0:npart, 0:width], func=AF.Sin), [b_ft], [wbuf])

        n_nonown_tiles = (NB - NOB) * 4
        half_bufs = {}
        pending_T = []
        tile_ctr = [0]
        ctx = {}

        class _Banks:
            def __init__(self, items):
                self.items = items
                self.i = 0

            def get1(self):
                it = self.items[self.i % len(self.items)]
                self.i += 1
                return it
        ps_tile = _Banks([(pairs[0][0][:, 0:512], pairs[0][1]), (pairs[0][0][:, 512:1024], pairs[0][2]),
                          (pairs[1][0][:, 0:512], pairs[1][1])])
        ps_ch = _Banks([(pairs[1][0][:, 512:1024], pairs[1][2])])

        def chain(b):
            own = b < NOB
            c0 = b * TB
            sq, b_sq = sq_pool.get()
            if own:
                xg = xg_own[:, :, c0:c0 + TB]
                b_xg = b_xgown[b]
            else:
                xgt, b_xg = xgb_pool.get()
                xg = xgt[:, :, :]
            for k in range(8):
                xs, b_xs = xs_pool.get()
                P.dma("sp", xs[:], xT[k * 128:(k + 1) * 128, c0:c0 + TB], [], [b_xs])
                act(lambda e, xs=xs, sq=sq, k=k: e.activation(out=sq[:, k, :], in_=xs[:], func=AF.Square), [b_xs], [b_sq])
                dve(lambda e, xs=xs, xg=xg, k=k: e.tensor_scalar(out=xg[:, k, :], in0=xs[:], scalar1=gmix[:, k:k + 1], scalar2=None,
                                                                op0=ALU.mult), [b_xs, b_const], [b_xg])
                if k % 2 == 1:
                    yield
            yield
            ps, b_ps = ps_ch.get1()
            mm(ps, [(ones_bf[:], sq[:, k, :]) for k in range(8)], [b_sq, b_const], b_ps)
            yield
            if own:
                rx = rstdx_own[:, c0:c0 + TB]
                b_rx = b_rstdx_own[b]
            else:
                rxt, b_rx = rx_pool.get()
                rx = rxt[:]
            rstd(rx, ps, float(D), 0, [b_ps], b_rx)
            yield
            ps, b_ps = ps_ch.get1()
            for j in range(4):
                mm(ps[:, j:j + 1], [(sq[:, k, j * 128:(j + 1) * 128], ones_bf[:, 0:1]) for k in range(8)], [b_sq, b_const], b_ps)
            yield
            rstd(rstd_tm[:, b * 4:b * 4 + 4], ps[:, 0:4], float(D), 0, [b_ps], b_rtm[b])
            pit, b_pi = pi_pool.get()
            prt, b_pr = pr_pool.get()
            P.dma("sp", pit[:], posd[0:1, c0:c0 + TB], [], [b_pi])
            dve(lambda e, prt=prt: e.memset(prt[:], 1.0), [], [b_pr])
            dve(lambda e, prt=prt, pit=pit: e.tensor_copy(out=prt[0:1, :], in_=pit[:]), [b_pi], [b_pr])
            yield
            ps, b_ps = ps_ch.get1()
            mm(ps, [(c2M[:], prt[:])], [b_c2, b_pr], b_ps)
            yield
            if own:
                trg = trigM_own[:, c0:c0 + TB]
                b_trg = b_trigMo[b]
            else:
                trgt, b_trg = trg_pool.get()
                trg = trgt[:]
            trig_from_angle(trg, ps, 128, [b_ps], b_trg, TB)
            yield
            ps, b_ps = ps_ch.get1()
            mm(ps, [(wckv[:, k, :], xg[:, k, :]) for k in range(8)], [b_wckv, b_xg], b_ps)
            yield
            ckv, b_ckv = f_pool.get()
            dve(lambda e, ckv=ckv, ps=ps, rx=rx: e.tensor_tensor(out=ckv[:], in0=ps, in1=rx, op=ALU.mult), [b_ps, b_rx], [b_ckv])
            sq2, b_sq2 = h_pool.get()
            act(lambda e, sq2=sq2, ckv=ckv: e.activation(out=sq2[:], in_=ckv[:], func=AF.Square), [b_ckv], [b_sq2])
            yield
            yield
            ps, b_ps = ps_ch.get1()
            mm(ps, [(ones_bf[:], sq2[:])], [b_sq2, b_const], b_ps)
            yield
            rkv, b_rkv = f_pool.get()
            rstd(rkv[:], ps, 128.0, 0, [b_ps], b_rkv)
            dve(lambda e, ckv=ckv, rkv=rkv, c0=c0: e.scalar_tensor_tensor(out=ckvn[:, c0:c0 + TB], in0=ckv[:], scalar=gkva[:, 0:1], in1=rkv[:],
                                                                          op0=ALU.mult, op1=ALU.mult), [b_ckv, b_rkv, b_const], [b_ckvn[b]])
            yield
            ps, b_ps = ps_ch.get1()
            mm(ps[0:64, :], [(wkr[:, k, :], xg[:, k, :]) for k in range(8)], [b_wkr, b_xg], b_ps)
            yield
            krw, b_krw = f_pool.get()
            dve(lambda e, krw=krw, ps=ps, rx=rx: e.tensor_tensor(out=krw[0:32, :], in0=ps[0:32, :], in1=rx[0:32, :], op=ALU.mult),
                [b_ps, b_rx], [b_krw])
            act(lambda e, krw=krw, c0=c0: e.activation(out=KRS[32:64, c0:c0 + TB], in_=krw[0:32, :], func=AF.Square), [b_krw], [b_KRS[b]])
            kt_, b_kt = h_pool.get()
            dve(lambda e, kt_=kt_, ps=ps, trg=trg: e.scalar_tensor_tensor(out=kt_[0:64, :], in0=ps[0:64, :], scalar=gkaug[0:64, 0:1],
                                                                           in1=trg[0:64, :], op0=ALU.mult, op1=ALU.mult),
                [b_ps, b_trg, b_const], [b_kt])
            yield
            yield
            ps2, b_ps2 = ps_ch.get1()
            mm(ps2[0:32, :], [(JMt[0:64, 0:32], kt_[0:64, :])], [b_kt, b_const], b_ps2)
            yield
            dve(lambda e, ps2=ps2, rx=rx, c0=c0: e.tensor_tensor(out=KRS[0:32, c0:c0 + TB], in0=ps2[0:32, :], in1=rx[0:32, :], op=ALU.mult),
                [b_ps2, b_rx], [b_KRS[b]])
            yield
            if own:
                cqf = []
                for c in range(2):
                    ps, b_ps = ps_ch.get1()
                    mm(ps, [(wcq[:, k, c * 128:(c + 1) * 128], xg[:, k, :]) for k in range(8)], [b_wcq, b_xg], b_ps)
                    yield
                    cq, b_cq = f_pool.get()
                    dve(lambda e, cq=cq, ps=ps, rx=rx: e.tensor_tensor(out=cq[:], in0=ps, in1=rx, op=ALU.mult), [b_ps, b_rx], [b_cq])
                    cqf.append((cq, b_cq))
                    yield
                sqs = []
                for c in range(2):
                    s_, b_s = h_pool.get()
                    act(lambda e, s_=s_, cq=cqf[c][0]: e.activation(out=s_[:], in_=cq[:], func=AF.Square), [cqf[c][1]], [b_s])
                    sqs.append((s_, b_s))
                yield
                yield
                ps, b_ps = ps_ch.get1()
                mm(ps, [(ones_bf[:], sqs[c][0][:]) for c in range(2)], [sqs[0][1], sqs[1][1], b_const], b_ps)
                yield
                rq, b_rq = f_pool.get()
                rstd(rq[:], ps, 256.0, 0, [b_ps], b_rq)
                for c in range(2):
                    dve(lambda e, c=c, cq=cqf[c][0], rq=rq, c0=c0: e.scalar_tensor_tensor(
                        out=cqn[:, c, c0:c0 + TB], in0=cq[:], scalar=gqa[:, c:c + 1], in1=rq[:], op0=ALU.mult, op1=ALU.mult),
                        [cqf[c][1], b_rq, b_const], [b_cqn[b]])
                yield
            else:
                bs, b_bs = bsb_pool.get()
                sfb, b_sfb = sfb_pool.get()
                dve(lambda e, bs=bs, b=b: e.tensor_tensor(out=bs[:, 0, :], in0=rstd_tm[:, b * 4:b * 4 + 4], in1=rstd_tm[:, b * 4:b * 4 + 4],
                                                          op=ALU.mult), [b_rtm[b]], [b_bs])
                dve(lambda e, bs=bs, b=b: e.tensor_tensor(out=bs[:, 1, :], in0=bs[:, 0, :], in1=mfb[:, 1, b * 4:b * 4 + 4], op=ALU.mult),
                    [b_bs, b_rel], [b_bs])
                dve(lambda e, bs=bs, b=b: e.tensor_tensor(out=bs[:, 0, :], in0=bs[:, 0, :], in1=mfb[:, 0, b * 4:b * 4 + 4], op=ALU.mult),
                    [b_bs, b_rel], [b_bs])
                dve(lambda e, bs=bs, sfb=sfb, b=b: e.tensor_tensor(out=sfb[:, 0, :, :], in0=wfall[:, :, b * 4:b * 4 + 4],
                                                                   in1=bs[:, 0, :].unsqueeze(1).broadcast_to([128, 8, 4]), op=ALU.mult),
                    [b_bs, b_rel], [b_sfb])
                dve(lambda e, bs=bs, sfb=sfb, b=b: e.tensor_tensor(out=sfb[:, 1, :, :], in0=wball[:, :, b * 4:b * 4 + 4],
                                                                   in1=bs[:, 1, :].unsqueeze(1).broadcast_to([128, 8, 4]), op=ALU.mult),
                    [b_bs, b_rel], [b_sfb])
                yield
                trtb, b_trtb = trtb_pool.get()
                for j in range(4):
                    psa, b_psa = ps_ch.get1()
                    mm(psa[:, 0:128], [(prt[:, j * 128:(j + 1) * 128], c2R[:])], [b_c2, b_pr], b_psa)
                    yield
                    trig_from_angle(trtb[:, j, :], psa[:, 0:128], 128, [b_psa], b_trtb, 128)
                    yield
                ctx[b] = (xg, b_xg, sfb, b_sfb, trtb, b_trtb)

        gens = []

        def adv(n=None):
            if n is None:
                n = 2 if len(gens) > 1 else 1
            for _ in range(n):
                while gens:
                    try:
                        next(gens[0])
                        break
                    except StopIteration:
                        gens.pop(0)

        def tiles(b):
            xg, b_xg, sfb, b_sfb, trtb, b_trtb = ctx.pop(b)
            for j in range(4):
                tc_ = tile_ctr[0]
                tile_ctr[0] += 1
                cs = slice(j * 128, (j + 1) * 128)
                ttok, _ = tt_pool.get()
                vf, _ = vf_pool.get()
                vb, _ = vb_pool.get()
                hb = [half_bufs.setdefault((id(t_), hh), Buf("half")) for t_ in (ttok, vf, vb) for hh in range(2)]
                b_tth, b_vfh, b_vbh = hb[0:2], hb[2:4], hb[4:6]
                first = tc_ == 0
                last = tc_ == n_nonown_tiles - 1

                def emit_T(hrange, ttok=ttok, vf=vf, vb=vb, b_tth=b_tth, b_vfh=b_vfh, b_vbh=b_vbh, first=first, last=last):
                    for h in hrange:
                        hs = slice(h * 128, (h + 1) * 128)
                        st_ = first and (h % 4 == 0)
                        P.op("pe", lambda e, hs=hs, st_=st_: e.matmul(
                            TFp[:, hs], lhsT=ttok[:, hs], rhs=vf[:, hs], start=st_, stop=last, skip_group_check=True),
                            [b_tth[h // 4], b_vfh[h // 4]], [bTFa if h < 4 else bTFb])
                        P.op("pe", lambda e, hs=hs, st_=st_: e.matmul(
                            TBp[:, hs], lhsT=ttok[:, hs], rhs=vb[:, hs], start=st_, stop=last, skip_group_check=True),
                            [b_tth[h // 4], b_vbh[h // 4]], [bTBa if h < 4 else bTBb])

                for hh in range(2):
                    psk, b_psk = ps_tile.get1()
                    mm(psk, [(xg[:, k, cs], wkall[:, k, hh * 512:(hh + 1) * 512]) for k in range(8)], [b_wkall, b_xg], b_psk, mid=adv)
                    dve(lambda e, ttok=ttok, psk=psk, hh=hh, j=j: e.tensor_tensor(
                        out=ttok[:, hh * 512:(hh + 1) * 512].rearrange("p (h c) -> p h c", h=4),
                        in0=psk.rearrange("p (h c) -> p h c", h=4),
                        in1=trtb[:, j, :].unsqueeze(1).broadcast_to([128, 4, 128]), op=ALU.mult), [b_psk, b_trtb], [b_tth[hh]])
                    adv()
                    psv, b_psv = ps_tile.get1()
                    mm(psv, [(xg[:, k, cs], wvall[:, k, hh * 512:(hh + 1) * 512]) for k in range(8)], [b_wvall, b_xg], b_psv, mid=adv)
                    dve(lambda e, vf=vf, psv=psv, hh=hh, j=j: e.tensor_tensor(
                        out=vf[:, hh * 512:(hh + 1) * 512].rearrange("p (h c) -> p h c", h=4),
                        in0=psv.rearrange("p (h c) -> p h c", h=4),
                        in1=sfb[:, 0, hh * 4:(hh + 1) * 4, j:j + 1].broadcast_to([128, 4, 128]), op=ALU.mult), [b_psv, b_sfb], [b_vfh[hh]])
                    dve(lambda e, vb=vb, psv=psv, hh=hh, j=j: e.tensor_tensor(
                        out=vb[:, hh * 512:(hh + 1) * 512].rearrange("p (h c) -> p h c", h=4),
                        in0=psv.rearrange("p (h c) -> p h c", h=4),
                        in1=sfb[:, 1, hh * 4:(hh + 1) * 4, j:j + 1].broadcast_to([128, 4, 128]), op=ALU.mult), [b_psv, b_sfb], [b_vbh[hh]])
                    adv()
                    if hh == 0 and pending_T:
                        pending_T.pop()()
                        adv()
                emit_T(range(0, 4))
                adv()
                pending_T.append(lambda emit_T=emit_T: emit_T(range(4, 8)))

        hide = {4: [5, 0], 5: [6, 1], 6: [7, 2], 7: [8, 3]}
        for b_ in range(8, 15):
            hide[b_] = [b_ + 1]
        hide[15] = []
        for _ in chain(NOB):
            pass
        for b in range(NOB, NB):
            for x_ in hide[b]:
                gens.append(chain(x_))
            tiles(b)
            while gens:
                adv()
        while pending_T:
            pending_T.pop()()
        act(lambda e: e.copy(out=TF0[:].rearrange("p h c -> p (h c)"), in_=TFp[:]), [bTFa, bTFb], [b_T0])
        act(lambda e: e.copy(out=TB0[:].rearrange("p h c -> p (h c)"), in_=TBp[:]), [bTBa, bTBb], [b_T0])

        if stop == '1':
            P.barrier()
            P.emit()
            return nc
        P.at_kb(128)
        wqb = P.sb("wqb", [128, 2, 1024], BF16)
        wkvbk = P.sb("wkvbk", [128, 8, 128], BF16)
        wkvbv = P.sb("wkvbv", [128, 8, 64], BF16)
        b_w3 = Buf("w3")
        P.dma("pool", wqb[:].rearrange("p a b -> p (a b)"), wqbd, [], [b_w3])
        dve(lambda e: e.memset(wkvbk[:], 0.0), [], [b_w3])
        P.dma("pool", wkvbk[:, :, 64:128], wkvbkd.rearrange("p (h c) -> p h c", h=8), [], [b_w3])
        P.dma("pool", wkvbv[:].rearrange("p a b -> p (a b)"), wkvbvd, [], [b_w3])
        Kh2 = [P.sb("Kh%d" % i, [128, NT], BF16) for i in range(2)]
        Qh2 = [P.sb("Qh%d" % i, [128, OWN], BF16) for i in range(2)]
        Vh = P.sb("Vh", [128, 64, 128], BF16)
        b_Kh2 = [Buf("Kh0"), Buf("Kh1")]
        b_Qh2 = [Buf("Qh0"), Buf("Qh1")]
        b_Vh = Buf("Vh")
        for i in range(2):
            dve(lambda e, i=i: e.memset(Kh2[i][32:64, :], 0.0), [], [b_Kh2[i]])
        dve(lambda e: e.memset(Vh[:, :, 64:128], 1.0), [], [b_Vh])
        f_pool = RPool(P, "ftmp3", [128, TB], F32, 3)
        h_pool = RPool(P, "htmp3", [128, TB], BF16, 3)
        pt_pool = RPool(P, "pt", [128, 1024], BF16, 3)
        oh_pool = RPool(P, "oh", [64, TB], BF16, 1)
        SCALE = 96.0 ** -0.5
        ps_sc = PsPool(pairs[0:3])

        class _One:
            def get1(self):
                return pairs[3][0][:, 512:1024], pairs[3][2]
        ps_pr = _One()
        po_bank = (pairs[3][0][:, 0:512], pairs[3][1])

        def prep_head(h):
            Kh = Kh2[h % 2]
            b_Kh = b_Kh2[h % 2]
            Qh = Qh2[h % 2]
            b_Qh = b_Qh2[h % 2]
            for b in range(NB):
                c0 = b * TB
                ps, b_ps = ps_pr.get1()
                mm(ps, [(wkvbk[:, h, :], ckvn[:, c0:c0 + TB])], [b_w3, b_ckvn[b]], b_ps)
                kn, b_kn = f_pool.get()
                dve(lambda e, kn=kn, ps=ps: e.tensor_copy(out=kn[64:128, :], in_=ps[64:128, :]), [b_ps], [b_kn])
                sqk, b_sqk = h_pool.get()
                pool(lambda e, sqk=sqk, kn=kn: e.tensor_tensor(out=sqk[64:128, :], in0=kn[64:128, :], in1=kn[64:128, :], op=ALU.mult),
                     [b_kn], [b_sqk])
                yield
                yield
                yield
                ps2, b_ps2 = ps_pr.get1()
                mm(ps2, [(ones_bf[64:128, :], sqk[64:128, :]), (ones_bf[32:64, :], KRS[32:64, c0:c0 + TB])],
                   [b_sqk, b_KRS[b], b_const], b_ps2)
                yield
                yield
                rk, b_rk = f_pool.get()
                rstd(rk[:], ps2, 96.0, 0, [b_ps2], b_rk)
                dve(lambda e, kn=kn, rk=rk, c0=c0, Kh=Kh: e.scalar_tensor_tensor(out=Kh[64:128, c0:c0 + TB], in0=kn[64:128, :],
                                                                                 scalar=gkaug[64:128, 0:1], in1=rk[64:128, :],
                                                                                 op0=ALU.mult, op1=ALU.mult),
                    [b_kn, b_rk, b_const], [b_Kh])
                dve(lambda e, rk=rk, c0=c0, Kh=Kh: e.tensor_tensor(out=Kh[0:32, c0:c0 + TB], in0=KRS[0:32, c0:c0 + TB], in1=rk[0:32, :],
                                                                   op=ALU.mult), [b_KRS[b], b_rk], [b_Kh])
                yield
            for b in range(NOB):
                c0 = b * TB
                ps, b_ps = ps_pr.get1()
                mm(ps, [(wqb[:, k, h * 128:(h + 1) * 128], cqn[:, k, c0:c0 + TB]) for k in range(2)], [b_w3, b_cqn[b]], b_ps)
                qa, b_qa = f_pool.get()
                dve(lambda e, qa=qa, ps=ps: e.tensor_copy(out=qa[:], in_=ps), [b_ps], [b_qa])
                sqq, b_sqq = h_pool.get()
                pool(lambda e, sqq=sqq, qa=qa: e.tensor_tensor(out=sqq[:], in0=qa[:], in1=qa[:], op=ALU.mult), [b_qa], [b_sqq])
                tq, b_tq = h_pool.get()
                dve(lambda e, tq=tq, qa=qa, c0=c0: e.scalar_tensor_tensor(out=tq[:], in0=qa[:], scalar=gqaug[:, 0:1], in1=trigM_own[:, c0:c0 + TB],
                                                                          op0=ALU.mult, op1=ALU.mult), [b_qa, b_trigMo[b], b_const], [b_tq])
                yield
                yield
                ps2, b_ps2 = ps_pr.get1()
                mm(ps2, [(maskq[:], sqq[:])], [b_sqq, b_const], b_ps2)
                yield
                yield
                rq, b_rq = f_pool.get()
                rstd(rq[:], ps2, 96.0, 0, [b_ps2], b_rq)
                yield
                yield
                ps3, b_ps3 = ps_pr.get1()
                mm(ps3, [(JMt[:], tq[:])], [b_tq, b_const], b_ps3)
                dve(lambda e, ps3=ps3, rq=rq, c0=c0, Qh=Qh: e.scalar_tensor_tensor(out=Qh[:, c0:c0 + TB], in0=ps3, scalar=SCALE, in1=rq[:],
                                                                                   op0=ALU.mult, op1=ALU.mult), [b_ps3, b_rq], [b_Qh])
                yield

        def prep_v(h):
            for g in range(8):
                ps, b_ps = ps_pr.get1()
                for j in range(8):
                    kt = g * 8 + j
                    mm(ps[:, j * 64:(j + 1) * 64], [(ckvn[:, kt * 128:(kt + 1) * 128], wkvbv[:, h, :])], [b_w3, b_ckvn[kt // 4]], b_ps)
                dve(lambda e, ps=ps, g=g: e.tensor_copy(out=Vh[:, g * 8:(g + 1) * 8, 0:64], in_=ps.rearrange("p (j c) -> p j c", j=8)),
                    [b_ps], [b_Vh])

        for _ in prep_head(0):
            pass
        for h in range(8):
            prep_v(h)
            Kh = Kh2[h % 2]
            b_Kh = b_Kh2[h % 2]
            Qh = Qh2[h % 2]
            b_Qh = b_Qh2[h % 2]
            nxt = prep_head(h + 1) if h < 7 else None
            items = [(qb, kp) for qb in range(NOB) for kp in range(32)]
            scs = {}

            def issue_scores(idx):
                qb, kp = items[idx]
                sc, b_s0, b_s1 = ps_sc.get2()
                for i, bsx in enumerate((b_s0, b_s1)):
                    kt = kp * 2 + i
                    mm(sc[:, i * 512:(i + 1) * 512], [(Kh[:, kt * 128:(kt + 1) * 128], Qh[:, qb * TB:(qb + 1) * TB])], [b_Kh, b_Qh], bsx)
                scs[idx] = (sc, b_s0, b_s1)

            issue_scores(0)
            issue_scores(1)
            for idx in range(len(items)):
                if idx + 2 < len(items):
                    issue_scores(idx + 2)
                qb, kp = items[idx]
                q0 = qb * TB
                po, b_po = po_bank
                sc, b_s0, b_s1 = scs.pop(idx)
                pt, b_pt = pt_pool.get()
                act(lambda e, pt=pt, sc=sc: e.activation(out=pt[:], in_=sc[:], func=AF.Exp), [b_s0, b_s1], [b_pt])
                for i in range(2):
                    kt = kp * 2 + i
                    P.op("pe", lambda e, po=po, kt=kt, pt=pt, i=i: e.matmul(po, lhsT=Vh[:, kt, :], rhs=pt[:, i * 512:(i + 1) * 512],
                                                                             start=(kt == 0), stop=(kt == 63)), [b_Vh, b_pt], [b_po])
                if kp == 31:
                    rec, b_rec = f_pool.get()
                    dve(lambda e, rec=rec, po=po: e.reciprocal(out=rec[0:64, :], in_=po[64:128, :]), [b_po], [b_rec])
                    if h % 2 == 0:
                        dve(lambda e, rec=rec, po=po, h=h, q0=q0: e.tensor_tensor(out=OA[0:64, h // 2, q0:q0 + TB], in0=po[0:64, :],
                                                                                  in1=rec[0:64, :], op=ALU.mult), [b_po, b_rec], [b_OA[qb]])
                    else:
                        oh, b_oh = oh_pool.get()
                        dve(lambda e, rec=rec, po=po, oh=oh: e.tensor_tensor(out=oh[:], in0=po[0:64, :], in1=rec[0:64, :], op=ALU.mult),
                            [b_po, b_rec], [b_oh])
                        dve(lambda e, oh=oh, h=h, q0=q0: e.tensor_copy(out=OA[64:128, h // 2, q0:q0 + TB], in_=oh[:]), [b_oh], [b_OA[qb]])
                if nxt is not None and idx >= 2:
                    next(nxt, None)
            if nxt is not None:
                for _ in nxt:
                    pass
        if stop == '3':
            P.barrier()
            P.emit()
            return nc
        P.at_kb(84)
        OB = P.sb("OB", [128, 8, OWN], BF16)
        b_OB = [Buf("OB%d" % i) for i in range(NOB)]
        trigR = P.sb("trigR", [128, OWN], BF16)
        trigT = P.sb("trigT", [128, 16, 128], BF16)
        b_trigR = [Buf("trigR%d" % i) for i in range(NOB)]
        b_trigT = [Buf("trigT%d" % i) for i in range(NOB)]
        f_pool = RPool(P, "ftmp2", [128, TB], F32, 4)
        h_pool = RPool(P, "htmp2", [128, TB], BF16, 4)
        it_pool = RPool(P, "itmp2", [128, TB], I32, 1)
        pr_pool = RPool(P, "posr2", [2, TB], F32, 1)
        pi_pool = RPool(P, "posi2", [1, TB], I32, 1)
        for b in range(NOB):
            c0 = b * TB
            pit, b_pi = pi_pool.get()
            prt, b_pr = pr_pool.get()
            P.dma("sp", pit[:], posd[0:1, c0:c0 + TB], [], [b_pi])
            dve(lambda e, prt=prt: e.memset(prt[:], 1.0), [], [b_pr])
            dve(lambda e, prt=prt, pit=pit: e.tensor_copy(out=prt[0:1, :], in_=pit[:]), [b_pi], [b_pr])
            ps, b_ps = ps_all.get1()
            mm(ps, [(c2R[:], prt[:])], [b_c2, b_pr], b_ps)
            trf, b_trf = f_pool.get()
            trig_from_angle(trf[:], ps, 128, [b_ps], b_trf, TB)
            dve(lambda e, c0=c0, trf=trf: e.tensor_tensor(out=trigR[:, c0:c0 + TB], in0=trf[:], in1=rstdx_own[:, c0:c0 + TB], op=ALU.mult),
                [b_trf, b_rstdx_own[b]], [b_trigR[b]])
            for j in range(4):
                ps, b_ps = ps_all.get1()
                mm(ps[:, 0:128], [(prt[:, j * 128:(j + 1) * 128], c2R[:])], [b_c2, b_pr], b_ps)
                trig_from_angle(trigT[:, b * 4 + j, :], ps[:, 0:128], 128, [b_ps], b_trigT[b], 128)
        wq_p = RPool(P, "wq_h", [128, 8, 128], BF16, 2)
        wk_p = RPool(P, "wk_h", [128, 8, 128], BF16, 2)
        wv_p = RPool(P, "wv_h", [128, 8, 128], BF16, 1)
        wg_p = RPool(P, "wg_h", [128, 8, 128], BF16, 1)
        QQ = P.sb("QQ", [128, OWN], BF16)
        QQF = P.sb("QQF", [128, OWN], BF16)
        QQB = P.sb("QQB", [128, OWN], BF16)
        KK = P.sb("KK", [128, OWN], BF16)
        SG = P.sb("SG", [128, OWN], BF16)
        Vt = P.sb("Vt", [128, 16, 128], BF16)
        VFt = P.sb("VFt", [128, 16, 128], BF16)
        VBt = P.sb("VBt", [128, 16, 128], BF16)
        TTt = P.sb("TTt", [128, 16, 128], BF16)
        SFb = P.sb("SFb", [128, 16, 128], BF16)
        SBb = P.sb("SBb", [128, 16, 128], BF16)
        Dh = P.sb("Dh", [128, 128], F32)
        dtmp = P.sb("dtmp", [128, 2, 128], F32)
        qdF = P.sb("qdF", [128, TB], F32)
        qdB = P.sb("qdB", [128, TB], F32)
        scol = P.sb("scol", [128, 3, 16], F32)
        stF = [P.sb("stF%d" % i, [128, 128], F32) for i in range(2)]
        stB = [P.sb("stB%d" % i, [128, 128], F32) for i in range(2)]
        ad_pool = RPool(P, "ad", [128, 128], BF16, 4)
        b_QQ = [Buf("QQ%d" % i) for i in range(NOB)]
        b_KK = [Buf("KK%d" % i) for i in range(NOB)]
        b_SG = [Buf("SG%d" % i) for i in range(NOB)]
        b_Vt = [Buf("Vt%d" % i) for i in range(16)]
        b_SF = [Buf("SF%d" % i) for i in range(16)]
        b_SB = [Buf("SB%d" % i) for i in range(16)]
        b_Dh = Buf("Dh")
        b_qd = Buf("qd")
        b_scol = Buf("scol")
        b_stF = [Buf("stF0"), Buf("stF1")]
        b_stB = [Buf("stB0"), Buf("stB1")]
        ps_ret = PsPool(pairs[0:3])
        po_ret = [(pairs[3][0][:, 0:512], pairs[3][1]), (pairs[3][0][:, 512:1024], pairs[3][2])]
        for h in range(8):
            wq, b_wq = wq_p.get()
            wk, b_wk = wk_p.get()
            wv, b_wv = wv_p.get()
            wg, b_wg = wg_p.get()
            for (t, bb, d_) in ((wq, b_wq, wqrd), (wk, b_wk, wkrrd), (wv, b_wv, wvrd), (wg, b_wg, wgrd)):
                P.dma("pool", t[:].rearrange("p a b -> p (a b)"), d_[:, h * 1024:(h + 1) * 1024], [], [bb])
            act(lambda e, h=h: e.activation(out=dtmp[:, 0, :], in_=dposm[:, 0, :], func=AF.Exp, scale=lgf[:, h:h + 1]), [b_dec], [b_Dh])
            act(lambda e, h=h: e.activation(out=dtmp[:, 1, :], in_=dposm[:, 1, :], func=AF.Exp, scale=lgb[:, h:h + 1]), [b_dec], [b_Dh])
            dve(lambda e: e.tensor_tensor(out=dtmp[:, 0, :], in0=dtmp[:, 0, :], in1=dposm[:, 2, :], op=ALU.mult), [b_Dh, b_dec], [b_Dh])
            dve(lambda e: e.tensor_tensor(out=dtmp[:, 1, :], in0=dtmp[:, 1, :], in1=dposm[:, 3, :], op=ALU.mult), [b_Dh, b_dec], [b_Dh])
            dve(lambda e: e.tensor_tensor(out=Dh[:], in0=dtmp[:, 0, :], in1=dtmp[:, 1, :], op=ALU.add), [b_Dh], [b_Dh])
            act(lambda e, h=h: e.activation(out=qdF[:], in_=cp1[:], func=AF.Exp, scale=lgf[:, h:h + 1]), [b_dec], [b_qd])
            act(lambda e, h=h: e.activation(out=qdB[:], in_=cm[:], func=AF.Exp, scale=lgb[:, h:h + 1]), [b_dec], [b_qd])
            dve(lambda e: e.tensor_copy(out=scol[:, 0, :], in_=rstd_tm[:, 0:16]), [b_rtm[i] for i in range(NOB)], [b_scol])
            dve(lambda e: e.tensor_tensor(out=scol[:, 2, :], in0=rstd_tm[:, 0:16], in1=rstd_tm[:, 0:16], op=ALU.mult),
                [b_rtm[i] for i in range(NOB)], [b_scol])
            dve(lambda e, h=h: e.tensor_scalar(out=scol[:, 1, :], in0=scol[:, 2, :], scalar1=dkf[:, h:h + 1], scalar2=0.125,
                                               op0=ALU.mult, op1=ALU.mult), [b_scol, b_dec], [b_scol])
            dve(lambda e, h=h: e.tensor_scalar(out=scol[:, 2, :], in0=scol[:, 2, :], scalar1=dkb[:, h:h + 1], scalar2=0.125,
                                               op0=ALU.mult, op1=ALU.mult), [b_scol, b_dec], [b_scol])
            ps_g = ps_ret
            for t in range(16):
                b = t // 4
                cs = slice(t * 128, (t + 1) * 128)
                ps, b_ps = ps_g.get1()
                mm(ps[:, 0:128], [(xg_own[:, k, cs], wk[:, k, :]) for k in range(8)], [b_wk, b_xgown[b]], b_ps)
                mm(ps[:, 128:256], [(xg_own[:, k, cs], wv[:, k, :]) for k in range(8)], [b_wv, b_xgown[b]], b_ps)
                dve(lambda e, ps=ps, t=t: e.tensor_tensor(out=TTt[:, t, :], in0=ps[:, 0:128], in1=trigT[:, t, :], op=ALU.mult),
                    [b_ps, b_trigT[b]], [b_Vt[t]])
                act(lambda e, ps=ps, t=t: e.activation(out=Vt[:, t, :], in_=ps[:, 128:256], func=AF.Copy, scale=scol[:, 0, t:t + 1]),
                    [b_ps, b_scol], [b_Vt[t]])
                act(lambda e, ps=ps, t=t: e.activation(out=VFt[:, t, :], in_=ps[:, 128:256], func=AF.Copy, scale=scol[:, 1, t:t + 1]),
                    [b_ps, b_scol], [b_Vt[t]])
                act(lambda e, ps=ps, t=t: e.activation(out=VBt[:, t, :], in_=ps[:, 128:256], func=AF.Copy, scale=scol[:, 2, t:t + 1]),
                    [b_ps, b_scol], [b_Vt[t]])

            def scans(h=h):
                act(lambda e: e.copy(out=stF[0][:], in_=TF0[:, h, :]), [b_T0], [b_stF[0]])
                act(lambda e: e.copy(out=stB[0][:], in_=TB0[:, h, :]), [b_T0], [b_stB[0]])
                act(lambda e: e.copy(out=SFb[:, 0, :], in_=TF0[:, h, :]), [b_T0], [b_SF[0]])
                act(lambda e: e.copy(out=SBb[:, 15, :], in_=TB0[:, h, :]), [b_T0], [b_SB[15]])
                pus = {}

                def issue_u(s_):
                    pu, b_pu = ps_g.get1()
                    tf, tb = s_, 15 - s_
                    mm(pu[:, 0:128], [(TTt[:, tf, :], VFt[:, tf, :])], [b_Vt[tf]], b_pu)
                    mm(pu[:, 128:256], [(TTt[:, tb, :], VBt[:, tb, :])], [b_Vt[tb]], b_pu)
                    pus[s_] = (pu, b_pu)
                issue_u(0)
                cur = 0
                for s_ in range(15):
                    if s_ + 1 < 15:
                        issue_u(s_ + 1)
                    pu, b_pu = pus.pop(s_)
                    dve(lambda e, pu=pu, cur=cur: e.scalar_tensor_tensor(out=stF[1 - cur][:], in0=stF[cur][:], scalar=gCf[:, h:h + 1],
                                                                         in1=pu[:, 0:128], op0=ALU.mult, op1=ALU.add),
                        [b_pu, b_stF[cur], b_dec], [b_stF[1 - cur]])
                    dve(lambda e, pu=pu, cur=cur: e.scalar_tensor_tensor(out=stB[1 - cur][:], in0=stB[cur][:], scalar=gCb[:, h:h + 1],
                                                                         in1=pu[:, 128:256], op0=ALU.mult, op1=ALU.add),
                        [b_pu, b_stB[cur], b_dec], [b_stB[1 - cur]])
                    cur = 1 - cur
                    act(lambda e, s_=s_, cur=cur: e.copy(out=SFb[:, s_ + 1, :], in_=stF[cur][:]), [b_stF[cur]], [b_SF[s_ + 1]])
                    act(lambda e, s_=s_, cur=cur: e.copy(out=SBb[:, 14 - s_, :], in_=stB[cur][:]), [b_stB[cur]], [b_SB[14 - s_]])
                    yield
            sc_gen = scans()

            for b in range(NOB):
                c0 = b * TB
                ps, b_ps = ps_g.get1()
                mm(ps, [(wq[:, k, :], xg_own[:, k, c0:c0 + TB]) for k in range(8)], [b_wq, b_xgown[b]], b_ps)
                tq, b_tq = h_pool.get()
                dve(lambda e, tq=tq, ps=ps, c0=c0: e.tensor_tensor(out=tq[:], in0=ps, in1=trigR[:, c0:c0 + TB], op=ALU.mult),
                    [b_ps, b_trigR[b]], [b_tq])
                psk, b_psk = ps_g.get1()
                mm(psk, [(wk[:, k, :], xg_own[:, k, c0:c0 + TB]) for k in range(8)], [b_wk, b_xgown[b]], b_psk)
                tk, b_tk = h_pool.get()
                dve(lambda e, tk=tk, psk=psk, c0=c0: e.tensor_tensor(out=tk[:], in0=psk, in1=trigR[:, c0:c0 + TB], op=ALU.mult),
                    [b_psk, b_trigR[b]], [b_tk])
                next(sc_gen, None)
                ps2, b_ps2 = ps_g.get1()
                mm(ps2, [(Jt[:], tq[:])], [b_tq, b_const], b_ps2)
                act(lambda e, ps2=ps2, c0=c0: e.copy(out=QQ[:, c0:c0 + TB], in_=ps2), [b_ps2], [b_QQ[b]])
                dve(lambda e, ps2=ps2, c0=c0: e.tensor_tensor(out=QQF[:, c0:c0 + TB], in0=ps2, in1=qdF[:], op=ALU.mult), [b_ps2, b_qd], [b_QQ[b]])
                dve(lambda e, ps2=ps2, c0=c0: e.tensor_tensor(out=QQB[:, c0:c0 + TB], in0=ps2, in1=qdB[:], op=ALU.mult), [b_ps2, b_qd], [b_QQ[b]])
                ps3, b_ps3 = ps_g.get1()
                mm(ps3, [(Jt[:], tk[:])], [b_tk, b_const], b_ps3)
                act(lambda e, ps3=ps3, c0=c0: e.activation(out=KK[:, c0:c0 + TB], in_=ps3, func=AF.Copy, scale=0.125), [b_ps3], [b_KK[b]])
                next(sc_gen, None)
                psg, b_psg = ps_g.get1()
                mm(psg, [(wg[:, k, :], xg_own[:, k, c0:c0 + TB]) for k in range(8)], [b_wg, b_xgown[b]], b_psg)
                zt, b_zt = f_pool.get()
                dve(lambda e, zt=zt, psg=psg, c0=c0: e.tensor_tensor(out=zt[:], in0=psg, in1=rstdx_own[:, c0:c0 + TB], op=ALU.mult),
                    [b_psg, b_rstdx_own[b]], [b_zt])
                act(lambda e, zt=zt, c0=c0: e.activation(out=SG[:, c0:c0 + TB], in_=zt[:], func=AF.Silu), [b_zt], [b_SG[b]])
                next(sc_gen, None)
                next(sc_gen, None)
            for _ in sc_gen:
                pass

            pas = {}

            def issue_pa(t):
                cs = slice(t * 128, (t + 1) * 128)
                pa, b_pa = ps_g.get1()
                mm(pa[:, 0:128], [(KK[0:64, cs], QQ[0:64, cs])], [b_KK[t // 4], b_QQ[t // 4]], b_pa)
                pas[t] = (pa, b_pa)

            fin = {}

            def fin1(b):
                po, b_po = po_ret[b % 2]
                sqo, b_sqo = h_pool.get()
                act(lambda e, sqo=sqo, po=po: e.activation(out=sqo[:], in_=po, func=AF.Square), [b_po], [b_sqo])
                fin[b] = [sqo, b_sqo]

            def fin2(b):
                sqo, b_sqo = fin[b]
                ps2, b_ps2 = ps_g.get1()
                mm(ps2, [(ones_bf[:], sqo[:])], [b_sqo, b_const], b_ps2)
                ro, b_ro = f_pool.get()
                rstd(ro[:], ps2, 128.0, 0, [b_ps2], b_ro)
                fin[b] = [ro, b_ro]

            def fin3(b, h=h):
                po, b_po = po_ret[b % 2]
                ro, b_ro = fin.pop(b)
                c0 = b * TB
                dve(lambda e, ro=ro, po=po: e.tensor_tensor(out=ro[:], in0=po, in1=ro[:], op=ALU.mult), [b_po, b_ro], [b_ro])
                dve(lambda e, ro=ro, c0=c0: e.tensor_tensor(out=OB[:, h, c0:c0 + TB], in0=ro[:], in1=SG[:, c0:c0 + TB], op=ALU.mult),
                    [b_ro, b_SG[b]], [b_OB[b]])

            issue_pa(0)
            for t in range(16):
                b, j = t // 4, t % 4
                cs = slice(t * 128, (t + 1) * 128)
                if t + 1 < 16:
                    issue_pa(t + 1)
                po, b_po = po_ret[b % 2]
                pa, b_pa = pas.pop(t)
                ad, b_ad = ad_pool.get()
                dve(lambda e, ad=ad, pa=pa: e.tensor_tensor(out=ad[:], in0=pa[:, 0:128], in1=Dh[:], op=ALU.mult), [b_pa, b_Dh], [b_ad])
                mm(po[:, j * 128:(j + 1) * 128], [(Vt[:, t, :], ad[:]), (SFb[:, t, :], QQF[:, cs]), (SBb[:, t, :], QQB[:, cs])],
                   [b_Vt[t], b_ad, b_SF[t], b_SB[t], b_QQ[b]], b_po)
                if j == 3:
                    fin1(b)
                if b >= 1 and j == 1:
                    fin2(b - 1)
                if b >= 1 and j == 3:
                    fin3(b - 1)
            fin2(3)
            fin3(3)
        if stop == '2':
            P.barrier()
            P.emit()
            return nc
        P.at_kb(116)
        wmla = P.sb("wmla", [128, 4, 1024], BF16)
        wret = P.sb("wret", [128, 8, 1024], BF16)
        wout = P.sb("wout", [128, 8, 1024], BF16)
        b_w4 = Buf("w4")
        for (t, d_) in ((wmla, wmlad), (wret, wretd), (wout, woutd)):
            P.dma("pool", t[:].rearrange("p a b -> p (a b)"), d_, [], [b_w4])
        wgate3 = wgated.rearrange("p (k n) -> p k n", k=8)
        wg_pool = RPool(P, "wgt", [128, 8, 256], BF16, 4)
        xs_pool = RPool(P, "xs4", [128, TB], F32, 3)
        f_pool = RPool(P, "ftmp4", [128, TB], F32, 5)
        mg_pool = RPool(P, "mg", [128, 8, TB], BF16, 1)
        x1_pool = RPool(P, "x1t", [128, TB], F32, 3)
        sq4_pool = RPool(P, "sq4", [128, TB], BF16, 2)
        _top = P.top
        P.top = 16640 + 76 * 1024
        rstd2 = P.sb("rstd2", [128, OWN], F32)
        P.top = _top
        b_rstd2 = [Buf("rstd2_%d" % i) for i in range(NOB)]
        ps_p4 = PsPool(pairs[0:3])
        pss4, b_pss4 = pairs[3][0][:, 0:512], pairs[3][1]
        x1_writes = []
        for b in range(NOB):
            c0 = b * TB
            mg, b_mg = mg_pool.get()
            for c in range(8):
                cs = slice(c * 128, (c + 1) * 128)
                wgt, b_wgt = wg_pool.get()
                P.dma("pool", wgt[:, :, 0:128], wgate3[:, :, c * 128:(c + 1) * 128], [], [b_wgt])
                P.dma("pool", wgt[:, :, 128:256], wgate3[:, :, 1024 + c * 128:1024 + (c + 1) * 128], [], [b_wgt])
                pa, b_pa = ps_p4.get1()
                mm(pa, [(wmla[:, pr_, cs], OA[:, pr_, c0:c0 + TB]) for pr_ in range(4)], [b_w4, b_OA[b]], b_pa)
                pb, b_pb = ps_p4.get1()
                mm(pb, [(wret[:, k, cs], OB[:, k, c0:c0 + TB]) for k in range(8)], [b_w4, b_OB[b]], b_pb)
                g1, b_g1 = f_pool.get()
                g2, b_g2 = f_pool.get()
                for (gt, bg, off) in ((g1, b_g1, 0), (g2, b_g2, 128)):
                    pg, b_pg = ps_p4.get1()
                    mm(pg, [(wgt[:, k, off:off + 128], xg_own[:, k, c0:c0 + TB]) for k in range(8)], [b_wgt, b_xgown[b]], b_pg)
                    dve(lambda e, gt=gt, pg=pg, c0=c0: e.tensor_tensor(out=gt[:], in0=pg, in1=rstdx_own[:, c0:c0 + TB], op=ALU.mult),
                        [b_pg, b_rstdx_own[b]], [bg])
                    act(lambda e, gt=gt: e.activation(out=gt[:], in_=gt[:], func=AF.Sigmoid), [bg], [bg])
                dve(lambda e, g1=g1, pa=pa: e.tensor_tensor(out=g1[:], in0=pa, in1=g1[:], op=ALU.mult), [b_pa, b_g1], [b_g1])
                dve(lambda e, g2=g2, pb=pb: e.tensor_tensor(out=g2[:], in0=pb, in1=g2[:], op=ALU.mult), [b_pb, b_g2], [b_g2])
                dve(lambda e, g1=g1, g2=g2, mg=mg, c=c: e.tensor_tensor(out=mg[:, c, :], in0=g1[:], in1=g2[:], op=ALU.add),
                    [b_g1, b_g2], [b_mg])
            for c in range(8):
                cs = slice(c * 128, (c + 1) * 128)
                pw, b_pw = ps_p4.get1()
                mm(pw, [(wout[:, k, cs], mg[:, k, :]) for k in range(8)], [b_w4, b_mg], b_pw)
                xs, b_xs = xs_pool.get()
                P.dma("sp", xs[:], xT[c * 128:(c + 1) * 128, c0:c0 + TB], [], [b_xs])
                x1, b_x1 = x1_pool.get()
                dve(lambda e, x1=x1, pw=pw, xs=xs: e.tensor_tensor(out=x1[:], in0=pw, in1=xs[:], op=ALU.add), [b_pw, b_xs], [b_x1])
                x1_writes.append(P.dma("sp", x1s[c * 128:(c + 1) * 128, c0:c0 + TB], x1[:], [b_x1], []))
                sq4, b_sq4 = sq4_pool.get()
                act(lambda e, x1=x1, sq4=sq4: e.activation(out=sq4[:], in_=x1[:], func=AF.Square), [b_x1], [b_sq4])
                P.op("pe", lambda e, sq4=sq4, c=c: e.matmul(pss4, lhsT=ones_bf[:], rhs=sq4[:], start=(c == 0), stop=(c == 7)),
                     [b_sq4, b_const], [b_pss4])
            rstd(rstd2[:, c0:c0 + TB], pss4, float(D), 0, [b_pss4], b_rstd2[b])
        P.wait_ops("sp", x1_writes)

        if stop == '4':
            P.barrier()
            P.emit()
            return nc
        P.at_kb(20)
        x1n = P.sb("x1n", [128, 8, OWN], BF16)
        b_x1n = [Buf("x1n%d" % i) for i in range(NOB)]
        wgu_p = RPool(P, "wgu", [128, 8, 256], BF16, 3)
        wdn_p = RPool(P, "wdn", [128, 8, 128], BF16, 3)
        f_pool = RPool(P, "ftmp5", [128, TB], F32, 3)
        assert P.top <= 16640 + 76 * 1024, P.top
        P.top = 16640 + 84 * 1024
        actT = P.sb("actT", [128, 8, OWN], BF16)
        yp = P.sb("yp", [128, 8, OWN], F32)
        b_yp = [[Buf("yp%d_%d" % (c, i)) for i in range(NOB)] for c in range(8)]
        xs_pool = RPool(P, "xs5", [128, TB], F32, 6)
        wgu3 = wgud.rearrange("p (k n) -> p k n", k=8)
        wdn3 = wdownd.rearrange("p (c n) -> p c n", c=NFC)
        for b in range(NOB):
            c0 = b * TB
            for c in range(8):
                xs, b_xs = xs_pool.get()
                P.dma("sp", xs[:], x1s[c * 128:(c + 1) * 128, c0:c0 + TB], [], [b_xs])
                dve(lambda e, xs=xs, c=c, c0=c0: e.scalar_tensor_tensor(out=x1n[:, c, c0:c0 + TB], in0=xs[:], scalar=gffn[:, c:c + 1],
                                                                        in1=rstd2[:, c0:c0 + TB], op0=ALU.mult, op1=ALU.mult),
                    [b_xs, b_const, b_rstd2[b]], [b_x1n[b]])
        out_writes = []
        groups = [(0, 8), (8, 15), (15, 22)]
        for g, (f0, f1) in enumerate(groups):
            ng = f1 - f0
            b_act = [Buf("act%d_%d" % (g, i)) for i in range(NOB)]
            for ci in range(ng):
                fc = f0 + ci
                w, b_w = wgu_p.get()
                P.dma("pool", w[:, :, 0:128], wgu3[:, :, fc * 128:(fc + 1) * 128], [], [b_w])
                P.dma("pool", w[:, :, 128:256], wgu3[:, :, FH + fc * 128:FH + (fc + 1) * 128], [], [b_w])
                for b in range(NOB):
                    c0 = b * TB
                    pg, b_pg = ps_all.get1()
                    mm(pg, [(w[:, k, 0:128], x1n[:, k, c0:c0 + TB]) for k in range(8)], [b_w, b_x1n[b]], b_pg)
                    pu, b_pu = ps_all.get1()
                    mm(pu, [(w[:, k, 128:256], x1n[:, k, c0:c0 + TB]) for k in range(8)], [b_w, b_x1n[b]], b_pu)
                    gt, b_gt = f_pool.get()
                    act(lambda e, gt=gt, pg=pg: e.activation(out=gt[:], in_=pg, func=AF.Silu), [b_pg], [b_gt])
                    dve(lambda e, gt=gt, pu=pu, ci=ci, c0=c0: e.tensor_tensor(out=actT[:, ci, c0:c0 + TB], in0=pu, in1=gt[:], op=ALU.mult),
                        [b_pu, b_gt], [b_act[b]])
            for c in range(8):
                wd, b_wd = wdn_p.get()
                P.dma("pool", wd[:, 0:ng, :], wdn3[:, f0:f1, c * 128:(c + 1) * 128], [], [b_wd])
                for b in range(NOB):
                    c0 = b * TB
                    py, b_py = ps_all.get1()
                    mm(py, [(wd[:, ci, :], actT[:, ci, c0:c0 + TB]) for ci in range(ng)], [b_wd, b_act[b]], b_py)
                    if g == 0:
                        xs, b_xs = xs_pool.get()
                        P.dma("sp", xs[:], x1s[c * 128:(c + 1) * 128, c0:c0 + TB], [], [b_xs])
                        dve(lambda e, py=py, xs=xs, c=c, c0=c0: e.tensor_tensor(out=yp[:, c, c0:c0 + TB], in0=py, in1=xs[:], op=ALU.add),
                            [b_py, b_xs], [b_yp[c][b]])
                    elif g == 1:
                        dve(lambda e, py=py, c=c, c0=c0: e.tensor_tensor(out=yp[:, c, c0:c0 + TB], in0=py, in1=yp[:, c, c0:c0 + TB], op=ALU.add),
                            [b_py, b_yp[c][b]], [b_yp[c][b]])
                    else:
                        ot, b_ot = f_pool.get()
                        dve(lambda e, py=py, ot=ot, c=c, c0=c0: e.tensor_tensor(out=ot[:], in0=py, in1=yp[:, c, c0:c0 + TB], op=ALU.add),
                            [b_py, b_yp[c][b]], [b_ot])
                        out_writes.append(P.dma("sp", yT[c * 128:(c + 1) * 128, c0:c0 + TB], ot[:], [b_ot], []))
        P.wait_ops("sp", out_writes)
        P.emit()
    return nc


def _kchunks(w):
    K, N = w.shape
    return np.ascontiguousarray(w.reshape(K // 128, 128, N).transpose(1, 0, 2)).reshape(128, -1)


def _prep_weights(inp):
    f = lambda a: np.asarray(a, dtype=np.float32)
    w_in = f(inp["w_in"])
    out = {}
    out["gmix"] = np.ascontiguousarray(f(inp["g_mix"]).reshape(8, 128).T)
    out["gffn"] = np.ascontiguousarray(f(inp["g_ffn"]).reshape(8, 128).T)
    out["gqa"] = np.ascontiguousarray(f(inp["g_q_a"]).reshape(2, 128).T)
    out["gkva"] = np.ascontiguousarray(f(inp["g_kv_a"]).reshape(1, 128).T)

    def gaug(g):
        g = f(g)
        return np.concatenate([g[64:96], g[80:96], g[64:80], g[0:64]]).reshape(128, 1).copy()
    out["gqaug"] = gaug(inp["g_qn"])
    out["gkaug"] = gaug(inp["g_kn"])
    out["decf"] = f(inp["ret_decay_fwd"]).copy()
    out["decb"] = f(inp["ret_decay_bwd"]).copy()
    out["wcq"] = _kchunks(w_in[:, 0:256])
    out["wckv"] = _kchunks(w_in[:, 256:384])
    kr = w_in[:, 384:416]
    out["wkr"] = _kchunks(np.concatenate([kr, kr[:, 16:32], kr[:, 0:16]], axis=1))

    def aug_r(w):
        w = w.reshape(1024, 8, 64)
        return np.concatenate([w, w[:, :, 32:64], w[:, :, 0:32]], axis=2)
    qa = aug_r(w_in[:, 416:928])
    ka = aug_r(w_in[:, 928:1440])
    vr = w_in[:, 1440:2464].reshape(1024, 8, 128)
    gr = w_in[:, 2464:3488].reshape(1024, 8, 128)

    def per_head(w):
        return np.ascontiguousarray(w.reshape(8, 128, 8, 128).transpose(1, 2, 0, 3)).reshape(128, -1)
    out["wqr"] = per_head(qa)
    out["wkrr"] = per_head(ka)
    out["wvr"] = per_head(vr)
    out["wgr"] = per_head(gr)
    out["wkall"] = _kchunks(ka.reshape(1024, 1024))
    out["wvall"] = _kchunks(vr.reshape(1024, 1024))
    out["wgate"] = _kchunks(w_in[:, 3488:5536])
    wqb = f(inp["w_q_b"]).reshape(256, 8, 96)
    wqb_aug = np.concatenate([wqb[:, :, 64:96], wqb[:, :, 80:96], wqb[:, :, 64:80], wqb[:, :, 0:64]], axis=2)
    out["wqb"] = _kchunks(wqb_aug.reshape(256, 1024))
    wkvb = f(inp["w_kv_b"]).reshape(128, 8, 128)
    out["wkvbk"] = np.ascontiguousarray(wkvb[:, :, 0:64]).reshape(128, -1)
    out["wkvbv"] = np.ascontiguousarray(wkvb[:, :, 64:128]).reshape(128, -1)
    out["wmla"] = _kchunks(f(inp["w_mla_out"]))
    out["wret"] = _kchunks(f(inp["w_ret_out"]))
    out["wout"] = _kchunks(f(inp["w_out"]))
    out["wgu"] = _kchunks(f(inp["w_gate_up"]))
    out["wdown"] = _kchunks(f(inp["w_down"]))
    return out


_NC_CACHE = {}


def kernel(**inputs):
    x = np.asarray(inputs["x"], dtype=np.float32)
    positions = np.asarray(inputs["positions"]).astype(np.int32)
    wts = _prep_weights(inputs)
    in_maps = []
    for c in range(8):
        b, j = c // 4, c % 4
        order = np.concatenate([np.arange(j * OWN, (j + 1) * OWN),
                                np.arange(0, j * OWN), np.arange((j + 1) * OWN, NT)])
        m = dict(wts)
        m["xT"] = np.ascontiguousarray(x[b][order].T)
        m["pos"] = np.ascontiguousarray(positions[b][order].reshape(1, NT))
        rel = (order - j * OWN).astype(np.float32)
        m["rel"] = np.ascontiguousarray(rel.reshape(64, 128).T)
        in_maps.append(m)
    if "nc" not in _NC_CACHE:
        _NC_CACHE["nc"] = build_nc()
    nc = _NC_CACHE["nc"]
    res = run_bass_kernel_spmd(nc, in_maps, core_ids=list(range(8)))
    out = np.empty((2, NT, D), dtype=np.float32)
    for c in range(8):
        b, j = c // 4, c % 4
        out[b, j * OWN:(j + 1) * OWN, :] = np.asarray(res.results[c]["yT"], dtype=np.float32).T
    return out
```
